# Optimizing a Trainium2 kernel written in Bass

```python
import math
import jax, jax.numpy as jnp
from jax import lax
import numpy as np

D_MODEL = 1024
BATCH = 8
SEQ = 4096
DEPTH = 1
DEC_BATCH = 16
DEC_SEQ = 4096
PAST_LEN = 128

N_HEADS = 16
N_KV_HEADS = 4
HEAD_DIM = 64
Q_PER_KV = N_HEADS // N_KV_HEADS
WINDOW = 128
BLOCK = 128
N_BUCKETS = 32
MAX_DISTANCE = 128
ATTN_Q = N_HEADS * HEAD_DIM
ATTN_KV = N_KV_HEADS * HEAD_DIM
NEG_INF = -1e30
SSM_WIDTH = D_MODEL
SSM_GROUP = 16
SSM_GROUPS = SSM_WIDTH // SSM_GROUP
SSM_STATE = 64
N_DIR = 2
DT_MIN = 0.001
DT_MAX = 0.1
D_FF = 4 * D_MODEL
EPS = 1e-6
IN_SPLITS = (ATTN_Q, ATTN_Q + ATTN_KV, ATTN_Q + 2 * ATTN_KV,
             ATTN_Q + 2 * ATTN_KV + SSM_WIDTH,
             ATTN_Q + 2 * ATTN_KV + SSM_WIDTH + D_MODEL)
IN_COLS = ATTN_Q + 2 * ATTN_KV + SSM_WIDTH + 2 * D_MODEL

kernel_name = "hybrid_s5_window_gqa_encoder"


def rms_norm(x, g):
    xf = x.astype(jnp.float32)
    y = xf * lax.rsqrt(jnp.mean(xf * xf, axis=-1, keepdims=True) + EPS)
    return (y * g.astype(jnp.float32)).astype(x.dtype)


def t5_bucket(rel):
    nb = N_BUCKETS // 2
    ret = (rel > 0).astype(np.int32) * nb
    n = np.abs(rel)
    max_exact = nb // 2
    n_safe = np.maximum(n, 1).astype(np.float32)
    large = max_exact + (np.log(n_safe / max_exact) / math.log(MAX_DISTANCE / max_exact)
                         * (nb - max_exact)).astype(np.int32)
    large = np.minimum(large, nb - 1)
    return (ret + np.where(n < max_exact, n, large)).astype(np.int32)


def band_bias(rel_table):
    qi = np.arange(BLOCK)[:, None]
    kj = np.arange(3 * BLOCK)[None, :]
    buckets = t5_bucket(kj - BLOCK - qi)
    bias = rel_table.astype(jnp.float32)[buckets]
    bias = jnp.transpose(bias, (2, 0, 1))
    return bias.reshape(N_KV_HEADS, Q_PER_KV, BLOCK, 3 * BLOCK)


def band_mask(seq_len):
    n_blocks = seq_len // BLOCK
    nb = jnp.arange(n_blocks)[:, None, None]
    qi = jnp.arange(BLOCK)[None, :, None]
    kj = jnp.arange(3 * BLOCK)[None, None, :]
    rel = kj - BLOCK - qi
    key_pos = nb * BLOCK - BLOCK + kj
    return (jnp.abs(rel) <= WINDOW) & (key_pos >= 0) & (key_pos < seq_len)


def windowed_attention(q, k, v, sink, bias, mask):
    seq_len = q.shape[1]
    n_blocks = seq_len // BLOCK
    scale = HEAD_DIM ** -0.5
    sink_b = sink.astype(jnp.float32).reshape(1, N_KV_HEADS, Q_PER_KV, 1, 1)

    def neighbours(t):
        tb = t.reshape(n_blocks, BLOCK, N_KV_HEADS, HEAD_DIM)
        tp = jnp.pad(tb, ((1, 1), (0, 0), (0, 0), (0, 0)))
        return jnp.concatenate([tp[:-2], tp[1:-1], tp[2:]], axis=1)

    def attend_one(args):
        qs, ks, vs = args
        qb = qs.reshape(n_blocks, BLOCK, N_KV_HEADS, Q_PER_KV, HEAD_DIM)
        kw, vw = neighbours(ks), neighbours(vs)
        s = jnp.einsum('nqkgd,njkd->nkgqj', qb, kw).astype(jnp.float32) * scale + bias
        s = jnp.where(mask[:, None, None], s, NEG_INF)
        sink_col = jnp.broadcast_to(sink_b, s.shape[:-1] + (1,))
        p = jax.nn.softmax(jnp.concatenate([s, sink_col], axis=-1), axis=-1)[..., :-1]
        o = jnp.einsum('nkgqj,njkd->nqkgd', p.astype(vs.dtype), vw)
        return o.reshape(seq_len, ATTN_Q)

    return lax.map(attend_one, (q, k, v))


def s5_discretize(lam_re, lam_im, log_dt, b_re, b_im):
    dt = jnp.exp(log_dt.astype(jnp.float32))[..., None]
    lr, li = lam_re.astype(jnp.float32), lam_im.astype(jnp.float32)
    mag = jnp.exp(lr * dt)
    ang = li * dt
    a_re, a_im = mag * jnp.cos(ang), mag * jnp.sin(ang)
    den = lr * lr + li * li
    nr, ni = a_re - 1.0, a_im
    c_re = (nr * lr + ni * li) / den
    c_im = (ni * lr - nr * li) / den
    br, bi = b_re.astype(jnp.float32), b_im.astype(jnp.float32)
    bb_re = c_re[..., None] * br - c_im[..., None] * bi
    bb_im = c_re[..., None] * bi + c_im[..., None] * br
    return a_re, a_im, bb_re, bb_im


def _complex_combine(e1, e2):
    a1r, a1i, b1r, b1i = e1
    a2r, a2i, b2r, b2i = e2
    return (a2r * a1r - a2i * a1i,
            a2r * a1i + a2i * a1r,
            a2r * b1r - a2i * b1i + b2r,
            a2r * b1i + a2i * b1r + b2i)


def s5_branch(u, lam_re, lam_im, log_dt, b_re, b_im, c_re, c_im, d_skip, w_glu, b_glu):
    a_re, a_im, bb_re, bb_im = s5_discretize(lam_re, lam_im, log_dt, b_re, b_im)
    cr, ci = c_re.astype(jnp.float32), c_im.astype(jnp.float32)
    dsk = d_skip.astype(jnp.float32)

    def one_direction(us, d, reverse):
        bu_re = jnp.einsum('lgc,gpc->lgp', us, bb_re[d])
        bu_im = jnp.einsum('lgc,gpc->lgp', us, bb_im[d])
        ar = jnp.broadcast_to(a_re[d], bu_re.shape)
        ai = jnp.broadcast_to(a_im[d], bu_re.shape)
        _, _, h_re, h_im = lax.associative_scan(_complex_combine, (ar, ai, bu_re, bu_im),
                                                reverse=reverse, axis=0)
        return (jnp.einsum('lgp,gcp->lgc', h_re, cr[d])
                - jnp.einsum('lgp,gcp->lgc', h_im, ci[d]))

    def one_seq(us):
        seq_len = us.shape[0]
        uf = us.astype(jnp.float32)
        ug = uf.reshape(seq_len, SSM_GROUPS, SSM_GROUP)
        y = one_direction(ug, 0, False) + one_direction(ug, 1, True)
        return (y.reshape(seq_len, SSM_WIDTH) + dsk * uf).astype(us.dtype)

    y = jax.nn.gelu(lax.map(one_seq, u))
    return y * jax.nn.sigmoid(y @ w_glu + b_glu)


def encoder_layer(x, bias, ln1, w_in, q_gain, k_gain, sink, lam_re, lam_im, log_dt,
                  b_re, b_im, c_re, c_im, d_skip, w_glu, b_glu, w_out, ln2, w_up, w_down):
    bsz, seq_len, _ = x.shape
    h = rms_norm(x, ln1)
    proj = h @ w_in
    q, k, v, u, g_att, g_ssm = jnp.split(proj, IN_SPLITS, axis=-1)
    q = rms_norm(q.reshape(bsz, seq_len, N_HEADS, HEAD_DIM), q_gain)
    k = rms_norm(k.reshape(bsz, seq_len, N_KV_HEADS, HEAD_DIM), k_gain)
    v = v.reshape(bsz, seq_len, N_KV_HEADS, HEAD_DIM)
    y_att = windowed_attention(q, k, v, sink, bias, band_mask(seq_len))
    y_ssm = s5_branch(u, lam_re, lam_im, log_dt, b_re, b_im, c_re, c_im,
                      d_skip, w_glu, b_glu)
    mixed = jax.nn.sigmoid(g_att) * y_att + jax.nn.sigmoid(g_ssm) * y_ssm
    x = x + mixed @ w_out
    h2 = rms_norm(x, ln2)
    return x + jnp.square(jax.nn.relu(h2 @ w_up)) @ w_down


def trunk(x, rel_table, ln1, w_in, q_gain, k_gain, sink, lam_re, lam_im, log_dt,
          b_re, b_im, c_re, c_im, d_skip, w_glu, b_glu, w_out, ln2, w_up, w_down):
    bias = band_bias(rel_table)
    for l in range(DEPTH):
        x = encoder_layer(x, bias, ln1[l], w_in[l], q_gain[l], k_gain[l], sink[l],
                          lam_re[l], lam_im[l], log_dt[l], b_re[l], b_im[l],
                          c_re[l], c_im[l], d_skip[l], w_glu[l], b_glu[l], w_out[l],
                          ln2[l], w_up[l], w_down[l])
    return x


def setup_inputs(seed: int = 0) -> dict:
    key = jax.random.key(seed)
    ks = jax.random.split(key, 24)
    nrm = jax.random.normal
    f32 = jnp.float32
    G, P, C = SSM_GROUPS, SSM_STATE, SSM_GROUP
    lam_im_base = jnp.broadcast_to(jnp.pi * jnp.arange(P, dtype=f32), (DEPTH, N_DIR, G, P))
    return {
        "x_prompt": nrm(ks[0], (BATCH, SEQ, D_MODEL), f32),
        "x_sample": nrm(ks[1], (DEC_BATCH, DEC_SEQ, D_MODEL), f32),
        "rel_table": 0.2 * nrm(ks[2], (N_BUCKETS, N_HEADS), f32),
        "ln1": 1.0 + 0.02 * nrm(ks[3], (DEPTH, D_MODEL), f32),
        "w_in": nrm(ks[4], (DEPTH, D_MODEL, IN_COLS), f32) * D_MODEL ** -0.5,
        "q_gain": 1.0 + 0.02 * nrm(ks[5], (DEPTH, HEAD_DIM), f32),
        "k_gain": 1.0 + 0.02 * nrm(ks[6], (DEPTH, HEAD_DIM), f32),
        "sink": 0.5 * nrm(ks[7], (DEPTH, N_HEADS), f32),
        "lam_re": -0.5 + 0.01 * nrm(ks[8], (DEPTH, N_DIR, G, P), f32),
        "lam_im": lam_im_base + 0.01 * nrm(ks[9], (DEPTH, N_DIR, G, P), f32),
        "log_dt": jax.random.uniform(ks[10], (DEPTH, N_DIR, G), f32,
                                     math.log(DT_MIN), math.log(DT_MAX)),
        "b_re": nrm(ks[11], (DEPTH, N_DIR, G, P, C), f32) * (2.0 * C) ** -0.5,
        "b_im": nrm(ks[12], (DEPTH, N_DIR, G, P, C), f32) * (2.0 * C) ** -0.5,
        "c_re": nrm(ks[13], (DEPTH, N_DIR, G, C, P), f32) * (2.0 * P) ** -0.5,
        "c_im": nrm(ks[14], (DEPTH, N_DIR, G, C, P), f32) * (2.0 * P) ** -0.5,
        "d_skip": nrm(ks[15], (DEPTH, SSM_WIDTH), f32),
        "w_glu": nrm(ks[16], (DEPTH, D_MODEL, D_MODEL), f32) * D_MODEL ** -0.5,
        "b_glu": 0.01 * nrm(ks[17], (DEPTH, D_MODEL), f32),
        "w_out": nrm(ks[18], (DEPTH, D_MODEL, D_MODEL), f32) * D_MODEL ** -0.5,
        "ln2": 1.0 + 0.02 * nrm(ks[19], (DEPTH, D_MODEL), f32),
        "w_up": nrm(ks[20], (DEPTH, D_MODEL, D_FF), f32) * D_MODEL ** -0.5,
        "w_down": nrm(ks[21], (DEPTH, D_FF, D_MODEL), f32) * D_FF ** -0.5,
    }


def reference(x_prompt, x_sample, rel_table, ln1, w_in, q_gain, k_gain, sink, lam_re, lam_im,
              log_dt, b_re, b_im, c_re, c_im, d_skip, w_glu, b_glu, w_out, ln2, w_up, w_down):
    y_prompt = trunk(x_prompt, rel_table, ln1, w_in, q_gain, k_gain, sink, lam_re, lam_im,
                     log_dt, b_re, b_im, c_re, c_im, d_skip, w_glu, b_glu, w_out, ln2,
                     w_up, w_down)
    y_sample = trunk(x_sample, rel_table, ln1, w_in, q_gain, k_gain, sink, lam_re, lam_im,
                     log_dt, b_re, b_im, c_re, c_im, d_skip, w_glu, b_glu, w_out, ln2,
                     w_up, w_down)
    return (y_prompt, y_sample)
```

```python
import contextlib
import math
import numpy as np
import concourse.bass as bass
import concourse.mybir as mybir
from concourse.bass_utils import run_bass_kernel_spmd

F32 = mybir.dt.float32
BF16 = mybir.dt.bfloat16
I32 = mybir.dt.int32
AF = mybir.ActivationFunctionType
ALU = mybir.AluOpType
AX = mybir.AxisListType

ENGS = ("tensor", "vector", "scalar", "gpsimd", "sync")

D = 1024
NG = 64
NP = 64
NC16 = 16
DFF = 4096
EPS = 1e-6
NEG = -30000.0


class Sem:
    def __init__(self, handle, inc, name):
        self.h = handle
        self.inc = inc
        self.n = 0
        self.name = name


class Buf:
    __slots__ = ("name", "w", "r")

    def __init__(self, name=""):
        self.name = name
        self.w = None
        self.r = {}


class Tracker:
    def __init__(self, nc, stack, n_dma_sems=8):
        self.nc = nc
        self.prog = {e: [] for e in ENGS}
        self.esem = {}
        for e in ENGS:
            self.esem[e] = Sem(stack.enter_context(nc.semaphore("s_" + e)), 1, e)
        self.seen = {e: {} for e in ENGS}
        self.dpool = {}
        self.dnext = {}
        for q in ("sync", "scalar", "gpsimd"):
            self.dpool[q] = [Sem(stack.enter_context(nc.semaphore("d_%s%d" % (q, i))), 16, "d_%s%d" % (q, i))
                             for i in range(n_dma_sems)]
            self.dnext[q] = 0
        self.pend = {e: ([], []) for e in ENGS}
        self.ninstr = 0

    def _wait(self, eng, sem, idx):
        if idx <= 0:
            return
        if self.seen[eng].get(sem, 0) >= idx:
            return
        self.seen[eng][sem] = idx
        h, v = sem.h, idx * sem.inc
        self.prog[eng].append(lambda E, h=h, v=v: E.wait_ge(h, v))
        self.ninstr += 1

    def _deps(self, eng, reads, writes):
        deps = {}

        def add(d):
            if d is None:
                return
            s, i = d
            if deps.get(s, 0) < i:
                deps[s] = i
        for b in reads:
            add(b.w)
        for b in writes:
            add(b.w)
            for s, i in b.r.items():
                add((s, i))
        for s, i in deps.items():
            self._wait(eng, s, i)

    def _check_pending(self, eng, reads, writes):
        for e in ENGS:
            if e == eng:
                continue
            pr, pw = self.pend[e]
            if not pr and not pw:
                continue
            for b in writes:
                assert b not in pr and b not in pw, "pending access on %s by %s" % (b.name, e)
            for b in reads:
                assert b not in pw, "pending write on %s by %s" % (b.name, e)

    def op(self, eng, fn, reads=(), writes=(), inc=True):
        reads = list(reads)
        writes = list(writes)
        self._check_pending(eng, reads, writes)
        self._deps(eng, reads, writes)
        sem = self.esem[eng]
        self.ninstr += 1
        if inc:
            sem.n += 1
            idx = sem.n
            h = sem.h
            self.prog[eng].append(lambda E, fn=fn, h=h: fn(E).then_inc(h, 1))
            pr, pw = self.pend[eng]
            for b in pw + writes:
                b.w = (sem, idx)
                b.r = {}
            for b in pr + reads:
                if b.w is not None and b.w[0] is sem and b.w[1] == idx:
                    continue
                b.r[sem] = idx
            self.pend[eng] = ([], [])
        else:
            self.prog[eng].append(lambda E, fn=fn: fn(E))
            pr, pw = self.pend[eng]
            pr.extend(reads)
            pw.extend(writes)

    def dma(self, q, out, in_, reads=(), writes=(), **kw):
        reads = list(reads)
        writes = list(writes)
        self._check_pending(q, reads, writes)
        pool = self.dpool[q]
        sem = pool[self.dnext[q] % len(pool)]
        self.dnext[q] += 1
        self._wait(q, sem, sem.n)
        self._deps(q, reads, writes)
        sem.n += 1
        idx = sem.n
        h = sem.h
        self.ninstr += 1
        self.prog[q].append(lambda E, out=out, in_=in_, h=h, kw=kw: E.dma_start(out=out, in_=in_, **kw).then_inc(h, 16))
        for b in writes:
            b.w = (sem, idx)
            b.r = {}
        for b in reads:
            b.r[sem] = idx

    def allsems(self):
        return list(self.esem.values()) + [s for p in self.dpool.values() for s in p]

    def barrier(self):
        for e in ENGS:
            assert not self.pend[e][0] and not self.pend[e][1]
        for e in ENGS:
            for s in self.allsems():
                if s is self.esem[e]:
                    continue
                self._wait(e, s, s.n)

    def finish(self, eng="sync"):
        for s in self.allsems():
            if s is self.esem[eng]:
                continue
            self._wait(eng, s, s.n)

    def replay(self):
        nc = self.nc
        with nc.Block() as block:
            @block.tensor
            def _(E):
                for f in self.prog["tensor"]:
                    f(E)

            @block.vector
            def _(E):
                for f in self.prog["vector"]:
                    f(E)

            @block.scalar
            def _(E):
                for f in self.prog["scalar"]:
                    f(E)

            @block.gpsimd
            def _(E):
                for f in self.prog["gpsimd"]:
                    f(E)

            @block.sync
            def _(E):
                for f in self.prog["sync"]:
                    f(E)


class Arena:
    def __init__(self, t, nbytes):
        self.t = t
        self.nbytes = nbytes
        self.off = 0

    def mark(self):
        return self.off

    def release(self, m):
        self.off = m

    def take(self, shape, dt):
        es = 4 if dt in (F32, I32) else 2
        n = int(np.prod(shape[1:]))
        nb = (n * es + 63) // 64 * 64
        assert self.off + nb <= self.nbytes, "arena overflow %d + %d > %d" % (self.off, nb, self.nbytes)
        v = self.t[0:int(shape[0]), self.off // 2:(self.off + n * es) // 2]
        self.off += nb
        if es == 4:
            v = v.bitcast(dt)
        if len(shape) > 2:
            names = " ".join("d%d" % i for i in range(len(shape) - 1))
            kw = {"d%d" % i: int(shape[i + 1]) for i in range(len(shape) - 1)}
            v = v.rearrange("p (%s) -> p %s" % (names, names), **kw)
        return v


def _t5_bucket(rel):
    nb = 16
    ret = (rel > 0).astype(np.int32) * nb
    n = np.abs(rel)
    max_exact = nb // 2
    n_safe = np.maximum(n, 1).astype(np.float32)
    large = max_exact + (np.log(n_safe / max_exact) / math.log(128 / max_exact) * (nb - max_exact)).astype(np.int32)
    large = np.minimum(large, nb - 1)
    return (ret + np.where(n < max_exact, n, large)).astype(np.int32)


def _consts():
    c = {}
    c["c_ident"] = np.eye(128, dtype=np.float32)
    sc = np.arange(128) // 16
    c["c_maskf"] = (sc[None, :] >= sc[:, None]).astype(np.float32)
    c["c_maskb"] = (sc[:, None] >= sc[None, :]).astype(np.float32)
    p = np.arange(128) % 64
    c["c_i2"] = (p[:, None] == np.arange(64)[None, :]).astype(np.float32)
    h = np.arange(128) // 64
    c["c_blk"] = (h[:, None] == h[None, :]).astype(np.float32)
    sg = np.where(np.arange(128) < 64, -1.0, 1.0).astype(np.float32)
    c["c_sgn"] = np.stack([sg, -sg], axis=1).astype(np.float32)
    j = np.arange(512)
    rel = j - 255
    bk = _t5_bucket(rel)
    oh = np.zeros((32, 512), np.float32)
    oh[bk[:511], j[:511]] = 1.0
    c["c_ohrev"] = oh
    key = np.arange(128)[:, None, None]
    bp = np.arange(3)[None, :, None]
    q = np.arange(128)[None, None, :]
    r = bp * 128 + key - 128 - q
    c["c_maskadd"] = np.where(np.abs(r) <= 128, 0.0, NEG).astype(np.float32)
    return c


def build(NSEQ, S, dbg=None):
    assert S % 1024 == 0
    NCH = S // 8
    NKT = S // 1024
    NBLK = S // 128
    NLEV = int(math.log2(NCH))
    assert 2 ** NLEV == NCH
    nc = bass.Bass("TRN2", target_bir_lowering=False)

    def din(name, shape, dt=F32):
        return nc.dram_tensor(name, list(shape), dt, kind="ExternalInput").ap()

    x_d = din("x", [NSEQ, S, D])
    y_d = nc.dram_tensor("y", [NSEQ, S, D], F32, kind="ExternalOutput").ap()
    rel_d = din("rel_table", [32, 16])
    ln1_d = din("ln1", [D])
    win_d = din("w_in", [D, 4608])
    qg_d = din("q_gain", [64])
    kg_d = din("k_gain", [64])
    sink_d = din("sink", [16])
    lre_d = din("lam_re", [2, NG, NP])
    lim_d = din("lam_im", [2, NG, NP])
    ldt_d = din("log_dt", [2, NG])
    bre_d = din("b_re", [2, NG, NP, NC16])
    bim_d = din("b_im", [2, NG, NP, NC16])
    cre_d = din("c_re", [2, NG, NC16, NP])
    cim_d = din("c_im", [2, NG, NC16, NP])
    dsk_d = din("d_skip", [D])
    wglu_d = din("w_glu", [D, D])
    bglu_d = din("b_glu", [D])
    wout_d = din("w_out", [D, D])
    ln2_d = din("ln2", [D])
    wup_d = din("w_up", [D, DFF])
    wdn_d = din("w_down", [DFF, D])
    cst = {k: din(k, v.shape) for k, v in _consts().items()}
    wx_s = nc.dram_tensor("wx_s", [128, 128, 128], BF16, kind="Internal").ap()
    wy_s = nc.dram_tensor("wy_s", [128, 128, 128], BF16, kind="Internal").ap()
    wt_s = nc.dram_tensor("wt_s", [64, 128, 128], BF16, kind="Internal").ap()
    vr_s = nc.dram_tensor("vr_s", [16, 512], F32, kind="Internal").ap()
    dbg_out = {}
    if dbg:
        for k, shp in dbg.items():
            dbg_out[k] = nc.dram_tensor("dbg_" + k, list(shp), F32, kind="ExternalOutput").ap()

    def bcast_rows(ap1d, n):
        return bass.AP(tensor=ap1d.tensor, offset=ap1d.offset, ap=[[0, 128], [1, n]])

    with contextlib.ExitStack() as st:
        T = Tracker(nc, st)

        def sb(name, shape, dt):
            return st.enter_context(nc.sbuf_tensor(name, list(shape), dt))

        def ps(name, shape, dt):
            return st.enter_context(nc.psum_tensor(name, list(shape), dt))

        UY = sb("UY", [128, NG, NCH], BF16)
        KT = sb("KT", [128, 2, S], BF16)
        VS = sb("VS", [128, NBLK, 4, 65], BF16)
        BE = sb("BE", [128, 3, 16, 128], BF16)
        WK = sb("WK", [128, 128, 9, 2], BF16)
        LN1 = sb("LN1", [128, D], BF16)
        LN2 = sb("LN2", [128, D], BF16)
        IDF = sb("IDF", [128, 128], F32)
        IDB = sb("IDB", [128, 128], BF16)
        I2 = sb("I2", [128, 64], F32)
        BLK = sb("BLK", [128, 128], BF16)
        SM = sb("SM", [128, 64], F32)
        ARENA_BYTES = 212863 - 1024 - (NG * NCH * 2 + 2 * S * 2 + NBLK * 4 * 65 * 2 + 3 * 16 * 128 * 2 + 128 * 18 * 2
                                       + 2 * D * 2 + 512 + 256 + 256 + 256 + 256)
        ARENA_BYTES = ARENA_BYTES // 64 * 64
        AR_T = sb("ARENA", [128, ARENA_BYTES // 2], BF16)
        AR = Arena(AR_T, ARENA_BYTES)
        PB = [ps("pb%d" % i, [128, 512], F32) for i in range(8)]
        PBb = [Buf("pb%d" % i) for i in range(8)]

        def pbf(i):
            return PB[i][:].bitcast(BF16)

        bUY = [Buf("UY%d" % g) for g in range(NG)]
        bKT = Buf("KT")
        bVS = Buf("VS")
        bRES = Buf("resident")

        def vop(fn, reads, writes, eng="vector"):
            T.op(eng, fn, reads, writes)

        def tt(out, a, b, op, reads, writes, eng="vector"):
            T.op(eng, lambda E: E.tensor_tensor(out, a, b, op), reads, writes)

        def ts(out, a, s1, s2, op0, op1, reads, writes, eng="vector"):
            if s2 is None:
                T.op(eng, lambda E: E.tensor_scalar(out, a, s1, None, op0), reads, writes)
            else:
                T.op(eng, lambda E: E.tensor_scalar(out, a, s1, s2, op0, op1), reads, writes)

        def stt(out, a, s, b, op0, op1, reads, writes):
            T.op("vector", lambda E: E.scalar_tensor_tensor(out, a, s, b, op0, op1), reads, writes)

        def cp(out, a, reads, writes, eng="vector"):
            T.op(eng, lambda E: E.tensor_copy(out, a), reads, writes)

        def act(out, a, func, reads, writes, **kw):
            T.op("scalar", lambda E: E.activation(out, a, func, **kw), reads, writes)

        def mm(out, lhsT, rhs, start, stop, reads, writes):
            T.op("tensor", lambda E: E.matmul(out, lhsT, rhs, start=start, stop=stop), reads, writes, inc=stop)

        def tr(out, in_, ident, reads, writes, inc=True):
            T.op("tensor", lambda E: E.transpose(out, in_, ident), reads, writes, inc=inc)

        def memset(ap, v, writes, eng="vector"):
            T.op(eng, lambda E: E.memset(ap, v), (), writes)

        def dbgdump(name, src_ap, reads):
            if name in dbg_out:
                T.dma("gpsimd", dbg_out[name], src_ap, reads=reads)

        bS = Buf("setup")
        R, W = [bS, bRES], [bS, bRES]
        m0 = AR.mark()
        T.dma("sync", IDF[:], cst["c_ident"], writes=W)
        T.dma("sync", I2[:], cst["c_i2"], writes=W)
        T.dma("gpsimd", IDB[:], cst["c_ident"], writes=W)
        T.dma("gpsimd", BLK[:], cst["c_blk"], writes=W)
        T.dma("gpsimd", LN1[:], bcast_rows(ln1_d, D), writes=W)
        T.dma("gpsimd", LN2[:], bcast_rows(ln2_d, D), writes=W)
        SGN = AR.take([128, 2], F32)
        T.dma("sync", SGN, cst["c_sgn"], writes=W)
        MKF = AR.take([128, 128], F32)
        MKB = AR.take([128, 128], F32)
        T.dma("sync", MKF, cst["c_maskf"], writes=W)
        T.dma("sync", MKB, cst["c_maskb"], writes=W)
        P1 = AR.take([128, 128, 8], F32)
        P2 = AR.take([128, 128, 8], F32)
        Q1y = AR.take([128, 128, 8], F32)
        Q2y = AR.take([128, 128, 8], F32)
        Q1h = AR.take([128, 128, 8], F32)
        Q2h = AR.take([128, 128, 8], F32)
        DV = AR.take([128, 64], F32)
        m1 = AR.mark()
        qg2 = bass.AP(tensor=qg_d.tensor, offset=0, ap=[[1, 64], [1, 1]])
        kg2 = bass.AP(tensor=kg_d.tensor, offset=0, ap=[[1, 64], [1, 1]])
        for hh in range(2):
            T.dma("sync", SM[hh * 64:(hh + 1) * 64, 0:1], qg2, writes=W)
            T.dma("sync", SM[hh * 64:(hh + 1) * 64, 1:2], kg2, writes=W)
        T.dma("sync", SM[:, 2:10], bglu_d.rearrange("(ct p) -> p ct", p=128), writes=W, allow_slow_non_contiguous=True)
        T.dma("sync", SM[:, 16:32], bcast_rows(sink_d, 16), writes=W)
        ts(SM[:, 0:1], SM[:, 0:1], 0.125, None, ALU.mult, None, R, W)
        act(SM[:, 16:32], SM[:, 16:32], AF.Exp, R, W)
        memset(VS[:, :, :, 64:65], 1.0, W)

        RT = AR.take([32, 16], F32)
        OH = AR.take([32, 512], F32)
        T.dma("sync", RT, rel_d, writes=W)
        T.dma("sync", OH, cst["c_ohrev"], writes=W)
        mm(PB[0][0:16, :], RT, OH, True, True, R, W + [PBb[0]])
        VR = AR.take([16, 512], F32)
        cp(VR, PB[0][0:16, :], R + [PBb[0]], W)
        T.dma("sync", vr_s, VR, reads=R, writes=W)
        MA = AR.take([128, 3, 128], F32)
        T.dma("sync", MA, cst["c_maskadd"], writes=W)
        BTMP = AR.take([128, 16, 128], F32)
        BTMP2 = AR.take([128, 16, 128], F32)
        for bp in range(3):
            src = bass.AP(tensor=vr_s.tensor, offset=bp * 128, ap=[[1, 128], [512, 16], [1, 128]])
            T.dma("sync", BTMP, src, reads=R, writes=W)
            tt(BTMP, BTMP[:, :, ::-1], MA[:, bp, :].unsqueeze(1).to_broadcast([128, 16, 128]), ALU.add, R, W) if False else None
            tt(BTMP2, BTMP[:, :, ::-1], MA[:, bp, :].unsqueeze(1).to_broadcast([128, 16, 128]), ALU.add, R, W)
            act(BE[:, bp, :, :], BTMP2, AF.Exp, R, W)

        AR.release(m1)
        def t128():
            return AR.take([128, 128], F32)
        LAM = AR.take([128, 2, 128], F32)
        lre2 = lre_d.rearrange("d g p -> (d g) p")
        lim2 = lim_d.rearrange("d g p -> (d g) p")
        for hh in range(2):
            T.dma("sync", LAM[:, 0, hh * 64:(hh + 1) * 64], lre2, writes=W)
            T.dma("sync", LAM[:, 1, hh * 64:(hh + 1) * 64], lim2, writes=W)
        LR, LI, DT = t128(), t128(), t128()
        mm(PB[0][:, 0:128], LAM[:, 0, :], IDF[:], True, True, R, W + [PBb[0]])
        cp(LR, PB[0][:, 0:128], R + [PBb[0]], W)
        mm(PB[1][:, 0:128], LAM[:, 1, :], IDF[:], True, True, R, W + [PBb[1]])
        cp(LI, PB[1][:, 0:128], R + [PBb[1]], W)
        T.dma("sync", DT, bcast_rows(ldt_d.rearrange("d g -> (d g)"), 128), writes=W)
        act(DT, DT, AF.Exp, R, W)
        LRDT, TU = t128(), t128()
        tt(LRDT, LR, DT, ALU.mult, R, W)
        tt(TU, LI, DT, ALU.mult, R, W)
        ts(TU, TU, 1.0 / (2 * math.pi), None, ALU.mult, None, R, W)
        MAG = t128()
        act(MAG, LRDT, AF.Exp, R, W)
        IM2 = t128()
        act(IM2, LRDT, AF.Exp, R, W, scale=-2.0)
        TI = AR.take([128, 128], I32)
        tA, tB, tC = t128(), t128(), t128()

        def reduce_turns(dst, src, shift):
            ts(tA, src, shift, None, ALU.add, None, R, W)
            cp(TI, tA, R, W)
            cp(tB, TI, R, W)
            tt(tA, tA, tB, ALU.subtract, R, W)
            ts(tB, tA, 0.5, None, ALU.is_gt, None, R, W)
            tt(tA, tA, tB, ALU.subtract, R, W)
            ts(tB, tA, -0.5, None, ALU.is_lt, None, R, W)
            tt(dst, tA, tB, ALU.add, R, W)

        A1R, A1I = t128(), t128()
        reduce_turns(tC, TU, 0.0)
        act(A1I, tC, AF.Sin, R, W, scale=2 * math.pi)
        reduce_turns(tC, TU, 0.25)
        act(A1R, tC, AF.Sin, R, W, scale=2 * math.pi)
        tt(A1R, A1R, MAG, ALU.mult, R, W)
        tt(A1I, A1I, MAG, ALU.mult, R, W)
        CBR, CBI = t128(), t128()
        ts(tA, A1R, -1.0, None, ALU.add, None, R, W)
        tt(tB, LR, LR, ALU.mult, R, W)
        tt(tC, LI, LI, ALU.mult, R, W)
        tt(tB, tB, tC, ALU.add, R, W)
        vop(lambda E: E.reciprocal(tB, tB), R, W)
        tt(CBR, tA, LR, ALU.mult, R, W)
        tt(tC, A1I, LI, ALU.mult, R, W)
        tt(CBR, CBR, tC, ALU.add, R, W)
        tt(CBR, CBR, tB, ALU.mult, R, W)
        tt(CBI, A1I, LR, ALU.mult, R, W)
        tt(tC, tA, LI, ALU.mult, R, W)
        tt(CBI, CBI, tC, ALU.subtract, R, W)
        tt(CBI, CBI, tB, ALU.mult, R, W)

        def cmul(zr, zi, xr, xi, yr, yi):
            tt(tA, xr, yr, ALU.mult, R, W)
            tt(tB, xi, yi, ALU.mult, R, W)
            tt(tC, xr, yi, ALU.mult, R, W)
            tt(zr, tA, tB, ALU.subtract, R, W)
            tt(tA, xi, yr, ALU.mult, R, W)
            tt(zi, tC, tA, ALU.add, R, W)

        PWR = AR.take([128, 16, 128], F32)
        PWI = AR.take([128, 16, 128], F32)

        def pw(n):
            return PWR[:, n + 7, :], PWI[:, n + 7, :]
        memset(PWR[:, 7, :], 1.0, W)
        memset(PWI[:, 7, :], 0.0, W)
        cp(PWR[:, 8, :], A1R, R, W)
        cp(PWI[:, 8, :], A1I, R, W)
        tt(PWR[:, 6, :], A1R, IM2, ALU.mult, R, W)
        tt(PWI[:, 6, :], A1I, IM2, ALU.mult, R, W)
        ts(PWI[:, 6, :], PWI[:, 6, :], -1.0, None, ALU.mult, None, R, W)
        for n in range(2, 9):
            cmul(*pw(n), *pw(n - 1), *pw(1))
        for n in range(2, 8):
            cmul(*pw(-n), *pw(-(n - 1)), *pw(-1))
        CLR, CLI, CNR, CNI = t128(), t128(), t128(), t128()
        cp(CLR, PWR[:, 15, :], R, W)
        cp(CLI, PWI[:, 15, :], R, W)
        for l in range(9):
            cp(WK[0:64, :, l, 0], CLR[0:64, :], R, W)
            ts(WK[64:128, :, l, 0], CLI[64:128, :], -1.0, None, ALU.mult, None, R, W)
            cp(WK[0:64, :, l, 1], CLI[0:64, :], R, W)
            cp(WK[64:128, :, l, 1], CLR[64:128, :], R, W)
            if l < 8:
                cmul(CNR, CNI, CLR, CLI, CLR, CLI)
                cp(CLR, CNR, R, W)
                cp(CLI, CNI, R, W)
        WXR, WXI = t128(), t128()
        for n in range(8):
            cmul(WXR, WXI, *pw(n), CBR, CBI)
            for (c0, c1, s) in ((0, 64, 7 - n), (64, 128, n)):
                cp(P1[:, c0:c1, s], WXR[:, c0:c1], R, W)
                ts(P2[:, c0:c1, s], WXI[:, c0:c1], SGN[:, 0:1], None, ALU.mult, None, R, W)
        for t in range(8):
            for (c0, c1, ey, eh) in ((0, 64, t + 1, t - 7), (64, 128, 8 - t, -t)):
                vr, vi = pw(ey)
                ts(Q1y[:, c0:c1, t], vr[:, c0:c1], SGN[:, 1:2], None, ALU.mult, None, R, W)
                ts(Q2y[:, c0:c1, t], vi[:, c0:c1], -1.0, None, ALU.mult, None, R, W)
                vr, vi = pw(eh)
                ts(Q1h[:, c0:c1, t], vr[:, c0:c1], SGN[:, 1:2], None, ALU.mult, None, R, W)
                ts(Q2h[:, c0:c1, t], vi[:, c0:c1], -1.0, None, ALU.mult, None, R, W)
        AR.release(m1)
        B1 = AR.take([128, 128, 16], F32)
        B2 = AR.take([128, 128, 16], F32)
        C1 = AR.take([128, 128, 16], F32)
        C2 = AR.take([128, 128, 16], F32)
        for d in range(2):
            sre = bre_d[d].rearrange("g p c -> p g c")
            sim = bim_d[d].rearrange("g p c -> p g c")
            gs = slice(d * 64, (d + 1) * 64)
            T.dma("sync", B1[0:64, gs, :], sre, writes=W)
            T.dma("sync", B1[64:128, gs, :], sim, writes=W)
            T.dma("sync", B2[0:64, gs, :], sim, writes=W)
            T.dma("sync", B2[64:128, gs, :], sre, writes=W)
        CN = AR.take([128, 16, 128], F32)
        cre2 = cre_d.rearrange("d (j g) c p -> (g c) (d j) p", g=8)
        cim2 = cim_d.rearrange("d (j g) c p -> (g c) (d j) p", g=8)
        for (CX, sa, sb_) in ((C1, cre2, cim2), (C2, cim2, cre2)):
            T.dma("sync", CN[:, :, 0:64], sa, writes=W)
            T.dma("sync", CN[:, :, 64:128], sb_, writes=W)
            for blk in range(16):
                pb = PB[blk % 2]
                mm(pb[:, 0:128], CN[:, blk, :], IDF[:], True, True, R, W + [PBb[blk % 2]])
                cp(CX[:, blk * 8:(blk + 1) * 8, :], pb[:, 0:128].rearrange("p (g c) -> p g c", g=8), R + [PBb[blk % 2]], W)
        dsk2 = dsk_d.rearrange("(g c) -> c g", c=16)
        for s in range(8):
            T.dma("sync", DV[s * 16:(s + 1) * 16, :], dsk2, writes=W, allow_slow_non_contiguous=True)
        NSL = 2
        XSs = [AR.take([128, 8, 16], F32) for _ in range(2 * NSL)]
        YSs = [AR.take([128, 8, 16], F32) for _ in range(2 * NSL)]
        YHs = [AR.take([128, 8, 16], F32) for _ in range(2 * NSL)]
        TMPs = [AR.take([128, 8, 16], F32) for _ in range(2 * NSL)]
        XTBs = [AR.take([128, 128], BF16) for _ in range(2 * NSL)]
        YSBs = [AR.take([128, 128], BF16) for _ in range(2 * NSL)]
        TFs = [AR.take([128, 128], F32) for _ in range(NSL)]
        TBs = [AR.take([128, 128], F32) for _ in range(NSL)]
        TTBs = [AR.take([128, 128], BF16) for _ in range(NSL)]
        bXS = [Buf("XS%d" % i) for i in range(2 * NSL)]
        bYS = [Buf("YS%d" % i) for i in range(2 * NSL)]
        bYH = [Buf("YH%d" % i) for i in range(2 * NSL)]
        bTMP = [Buf("TMP%d" % i) for i in range(2 * NSL)]
        bXTB = [Buf("XTB%d" % i) for i in range(2 * NSL)]
        bYSB = [Buf("YSB%d" % i) for i in range(2 * NSL)]
        bTF = [Buf("TF%d" % i) for i in range(NSL)]
        bSCRW = Buf("scrw")
        RT_ = [bS]

        def outer(dst, bdst, Pa, Ba, Pb, Bb, gd, tmp, btmp):
            tt(dst, Pa[:, gd, :].unsqueeze(2).to_broadcast([128, 8, 16]),
               Ba[:, gd, :].unsqueeze(1).to_broadcast([128, 8, 16]), ALU.mult, RT_, [bdst])
            tt(tmp, Pb[:, gd, :].unsqueeze(2).to_broadcast([128, 8, 16]),
               Bb[:, gd, :].unsqueeze(1).to_broadcast([128, 8, 16]), ALU.mult, RT_, [btmp])
            tt(dst, dst, tmp, ALU.add, [bdst, btmp], [bdst])

        def flat(v):
            return v.rearrange("p a b -> p (a b)")
        for g in range(NG):
            sl = g % NSL
            for d in range(2):
                gd = d * 64 + g
                k = sl * 2 + d
                outer(XSs[k], bXS[k], P1, B1, P2, B2, gd, TMPs[k], bTMP[k])
                outer(YSs[k], bYS[k], Q1y, C1, Q2y, C2, gd, TMPs[k], bTMP[k])
                outer(YHs[k], bYH[k], Q1h, C1, Q2h, C2, gd, TMPs[k], bTMP[k])
                pxt = k
                mm(PB[pxt][:, 0:128], flat(XSs[k]), IDF[:], True, True, [bXS[k]] + RT_, [PBb[pxt]])
                cp(XTBs[k], PB[pxt][:, 0:128], [PBb[pxt]], [bXTB[k]])
                T.dma("sync", wx_s[gd], XTBs[k], reads=[bXTB[k]], writes=[bSCRW])
                act(YSBs[k], flat(YSs[k]), AF.Copy, [bYS[k]], [bYSB[k]])
                T.dma("sync", wy_s[gd], YSBs[k], reads=[bYSB[k]], writes=[bSCRW])
                ptp = 4 + k
                mm(PB[ptp][:, 0:128], flat(XSs[k]), flat(YHs[k]), True, True, [bXS[k], bYH[k]], [PBb[ptp]])
            tt(TFs[sl], PB[4 + sl * 2][:, 0:128], MKF, ALU.mult, [PBb[4 + sl * 2]] + RT_, [bTF[sl]])
            tt(TBs[sl], PB[5 + sl * 2][:, 0:128], MKB, ALU.mult, [PBb[5 + sl * 2]] + RT_, [bTF[sl]])
            tt(TFs[sl], TFs[sl], TBs[sl], ALU.add, [bTF[sl]], [bTF[sl]])
            stt(TTBs[sl], IDF[:], DV[:, g:g + 1], TFs[sl], ALU.mult, ALU.add, [bTF[sl]] + RT_, [bTF[sl]])
            T.dma("sync", wt_s[g], TTBs[sl], reads=[bTF[sl]], writes=[bSCRW])
        AR.release(m0)
        T.barrier()
        bSCR = Buf("scratchw")

        def load_w(dst, src_rows, c0, ncols, kt, bw):
            T.dma("gpsimd", dst, src_rows[:, c0:c0 + ncols].rearrange("(k p) c -> p k c", p=128), writes=[bw])

        def rms_block(xt, lnb, hb, sm, bx, bh, bsm):
            memset(sm[:, 0:1], 0.0, [bsm])
            act(hb, xt, AF.Square, [bx], [bh, bsm], accum_out=sm[:, 0:1])
            act(sm[:, 1:2], sm[:, 0:1], AF.Ln, [bsm], [bsm], scale=1.0 / D, bias=EPS)
            act(sm[:, 2:3], sm[:, 1:2], AF.Exp, [bsm], [bsm], scale=-0.5)
            stt(hb, xt, sm[:, 2:3], lnb, ALU.mult, ALU.mult, [bx, bsm, bRES], [bh])

        for sq in range(NSEQ):
            mA = AR.mark()
            WU = AR.take([128, 8, 1024], BF16)
            WKV = AR.take([128, 8, 512], BF16)
            bWU, bWKV = Buf("WU"), Buf("WKV")
            load_w(WU, win_d, 1536, 1024, 8, bWU)
            load_w(WKV, win_d, 1024, 512, 8, bWKV)
            HT = AR.take([128, 8, 1024], BF16)
            bHT = Buf("HT")
            U8 = AR.take([128, NG, 8, 16], BF16)
            bU8 = Buf("U8")
            XB = [AR.take([128, D], F32) for _ in range(2)]
            bXB = [Buf("XB0"), Buf("XB1")]
            HB = [AR.take([128, D], BF16) for _ in range(2)]
            bHB = [Buf("HB0"), Buf("HB1")]
            SMA = [AR.take([128, 16], F32) for _ in range(2)]
            bSMA = [Buf("SMA0"), Buf("SMA1")]
            KSQ = AR.take([128, 512], BF16)
            KRS = AR.take([128, 512], F32)
            bK = Buf("ktmp")
            for kt in range(NKT):
                for b8 in range(8):
                    blk = kt * 8 + b8
                    i = blk % 2
                    T.dma("sync", XB[i], x_d[sq, blk * 128:(blk + 1) * 128, :], writes=[bXB[i]])
                    rms_block(XB[i], LN1[:], HB[i], SMA[i], bXB[i], bHB[i], bSMA[i])
                    pt = pbf(0)
                    for dt in range(8):
                        tr(pt[:, dt * 128:(dt + 1) * 128], HB[i][:, dt * 128:(dt + 1) * 128], IDB[:],
                           [bHB[i], bRES], [PBb[0]], inc=(dt == 7))
                    cp(HT[:, :, b8 * 128:(b8 + 1) * 128], pt.rearrange("p (a b) -> p a b", a=8), [PBb[0]], [bHT],
                       eng=("scalar" if False else "vector"))
                for half in range(2):
                    tok = slice(half * 512, (half + 1) * 512)
                    gtok = slice(kt * 1024 + half * 512, kt * 1024 + (half + 1) * 512)
                    for j in range(2):
                        pk = 1 + j
                        for dt in range(8):
                            mm(PB[pk][:], WKV[:, dt, j * 128:(j + 1) * 128], HT[:, dt, tok], dt == 0, dt == 7,
                               [bWKV, bHT], [PBb[pk]])
                        act(KSQ, PB[pk][:], AF.Square, [PBb[pk]], [bK])
                        mm(PB[3][:], BLK[:], KSQ, True, True, [bK, bRES], [PBb[3]])
                        act(KRS, PB[3][:], AF.Ln, [PBb[3]], [bK], scale=1.0 / 64, bias=EPS)
                        act(KRS, KRS, AF.Exp, [bK], [bK], scale=-0.5)
                        stt(KT[:, j, gtok], PB[pk][:], SM[:, 1:2], KRS, ALU.mult, ALU.mult, [PBb[pk], bK, bRES], [bKT])
                    for b4 in range(4):
                        blk = kt * 8 + half * 4 + b4
                        tk = slice(half * 512 + b4 * 128, half * 512 + (b4 + 1) * 128)
                        for dt in range(8):
                            mm(PB[4][:, 0:256], HT[:, dt, tk], WKV[:, dt, 256:512], dt == 0, dt == 7,
                               [bWKV, bHT], [PBb[4]])
                        act(VS[:, blk, :, 0:64], PB[4][:, 0:256].rearrange("p (h d) -> p h d", h=4), AF.Copy,
                            [PBb[4]], [bVS])
                for s in range(8):
                    for half in range(2):
                        pu = 5 + (s * 2 + half) % 2
                        for dt in range(8):
                            mm(PB[pu][:], HT[:, dt, s::8], WU[:, dt, half * 512:(half + 1) * 512], dt == 0, dt == 7,
                               [bWU, bHT], [PBb[pu]])
                        dst = U8[:, half * 32:(half + 1) * 32, s, :]
                        src = PB[pu][:].rearrange("p (g c) -> p g c", c=16)
                        if (s * 2 + half) % 2 == 0:
                            act(dst, src, AF.Copy, [PBb[pu]], [bU8])
                        else:
                            cp(dst, src, [PBb[pu]], [bU8])
                for g8 in range(8):
                    pt = pbf(7)
                    for gg in range(8):
                        g = g8 * 8 + gg
                        tr(pt[:, gg * 128:(gg + 1) * 128], U8[:, g, :, :].rearrange("p a b -> p (a b)"), IDB[:],
                           [bU8, bRES], [PBb[7]], inc=(gg == 7))
                    dst = UY[:, g8 * 8:(g8 + 1) * 8, kt * 128:(kt + 1) * 128]
                    src = pt.rearrange("p (a b) -> p a b", a=8)
                    if g8 % 2 == 0:
                        act(dst, src, AF.Copy, [PBb[7]], bUY[g8 * 8:(g8 + 1) * 8])
                    else:
                        cp(dst, src, [PBb[7]], bUY[g8 * 8:(g8 + 1) * 8])
            if sq == 0:
                dbgdump("UY", UY[:].rearrange("p a b -> p (a b)")[:, 0:dbg["UY"][1]] if dbg and "UY" in dbg else None, bUY)
                dbgdump("KT", KT[:].rearrange("p a b -> p (a b)"), [bKT])
            AR.release(mA)
            T.barrier()

            mS = AR.mark()
            NB = 2
            NSET = 2
            HBs = [[[AR.take([128, NCH + 2], BF16) for _ in range(2)] for _ in range(2 * NB)] for _ in range(NSET)]
            bHs = [[[Buf("H%d_%d_%d" % (st_, c, i)) for i in range(2)] for c in range(2 * NB)] for st_ in range(NSET)]
            WXss = [[AR.take([128, 128], BF16) for _ in range(2 * NB)] for _ in range(NSET)]
            WYss = [[AR.take([128, 128], BF16) for _ in range(2 * NB)] for _ in range(NSET)]
            WTss = [[AR.take([128, 128], BF16) for _ in range(NB)] for _ in range(NSET)]
            RTss = [[AR.take([128, 9, 2, 64], BF16) for _ in range(2 * NB)] for _ in range(NSET)]
            bWss = [[Buf("Wc%d_%d" % (st_, c)) for c in range(2 * NB)] for st_ in range(NSET)]
            bWTs = [[Buf("WT%d_%d" % (st_, c)) for c in range(NB)] for st_ in range(NSET)]
            bRTs = [[Buf("RT%d_%d" % (st_, c)) for c in range(2 * NB)] for st_ in range(NSET)]
            for st_ in range(NSET):
                for c in range(2 * NB):
                    for i in range(2):
                        memset(HBs[st_][c][i][:, 0:1], 0.0, [bHs[st_][c][i]])
                        memset(HBs[st_][c][i][:, NCH + 1:NCH + 2], 0.0, [bHs[st_][c][i]])

            def ssm_prefetch(bi):
                st_ = bi % NSET
                g0_ = bi * NB
                for gi in range(NB):
                    g = g0_ + gi
                    T.dma("sync", WTss[st_][gi], wt_s[g], reads=[bSCR], writes=[bWTs[st_][gi]])
                    for d in range(2):
                        c = gi * 2 + d
                        gd = d * 64 + g
                        T.dma("sync", WXss[st_][c], wx_s[gd], reads=[bSCR], writes=[bWss[st_][c]])
                        T.dma("sync", WYss[st_][c], wy_s[gd], reads=[bSCR], writes=[bWss[st_][c]])
                        tt(RTss[st_][c], I2[:].unsqueeze(1).unsqueeze(1).to_broadcast([128, 9, 2, 64]),
                           WK[:, gd, :, :].unsqueeze(3).to_broadcast([128, 9, 2, 64]), ALU.mult,
                           [bRES], [bRTs[st_][c]], eng="gpsimd")

            ssm_prefetch(0)
            for g0 in range(0, NG, NB):
                bi = g0 // NB
                st_ = bi % NSET
                HB_, bH = HBs[st_], bHs[st_]
                WXs, WYs, WTs, RTs = WXss[st_], WYss[st_], WTss[st_], RTss[st_]
                bWs, bWT, bRT = bWss[st_], bWTs[st_], bRTs[st_]
                chains = [(gi * 2 + d, g0 + gi, d) for gi in range(NB) for d in range(2)]
                for (c, g, d) in chains:
                    mm(PB[c][:, 0:NCH], WXs[c], UY[:, g, :], True, True, [bWs[c], bUY[g]], [PBb[c]])
                if g0 + NB < NG:
                    ssm_prefetch(bi + 1)
                for (c, g, d) in chains:
                    if c % 2 == 0:
                        act(HB_[c][0][:, 1:NCH + 1], PB[c][:, 0:NCH], AF.Copy, [PBb[c]], [bH[c][0]])
                    else:
                        cp(HB_[c][0][:, 1:NCH + 1], PB[c][:, 0:NCH], [PBb[c]], [bH[c][0]])
                cur = [0] * (2 * NB)
                for l in range(NLEV):
                    m = 1 << l
                    for (c, g, d) in chains:
                        Hc = HB_[c][cur[c]]
                        rt = RTs[c][:, l, :, :].rearrange("p a b -> p (a b)")
                        if c % 2 == 0:
                            mm(PB[c][:, 0:NCH], IDB[:], Hc[:, 1:NCH + 1], True, False, [bRES, bH[c][cur[c]]], [PBb[c]])
                            if d == 0:
                                mm(PB[c][:, m:NCH], rt, Hc[:, 1:NCH + 1 - m], False, True, [bRT[c], bH[c][cur[c]]], [PBb[c]])
                            else:
                                mm(PB[c][:, 0:NCH - m], rt, Hc[:, 1 + m:NCH + 1], False, True, [bRT[c], bH[c][cur[c]]], [PBb[c]])
                        else:
                            if d == 0:
                                mm(PB[c][:, m:NCH], rt, Hc[:, 1:NCH + 1 - m], True, True, [bRT[c], bH[c][cur[c]]], [PBb[c]])
                            else:
                                mm(PB[c][:, 0:NCH - m], rt, Hc[:, 1 + m:NCH + 1], True, True, [bRT[c], bH[c][cur[c]]], [PBb[c]])
                    for (c, g, d) in chains:
                        if c % 2 == 0:
                            Hn = HB_[c][1 - cur[c]]
                            act(Hn[:, 1:NCH + 1], PB[c][:, 0:NCH], AF.Copy, [PBb[c]], [bH[c][1 - cur[c]]])
                            cur[c] = 1 - cur[c]
                        else:
                            Hc = HB_[c][cur[c]]
                            if d == 0:
                                tt(Hc[:, 1 + m:NCH + 1], PB[c][:, m:NCH], Hc[:, 1 + m:NCH + 1], ALU.add,
                                   [PBb[c], bH[c][cur[c]]], [bH[c][cur[c]]])
                            else:
                                tt(Hc[:, 1:NCH + 1 - m], PB[c][:, 0:NCH - m], Hc[:, 1:NCH + 1 - m], ALU.add,
                                   [PBb[c], bH[c][cur[c]]], [bH[c][cur[c]]])
                for gi in range(NB):
                    g = g0 + gi
                    cf, cb = gi * 2, gi * 2 + 1
                    po = 4 + gi
                    for kt in range(NKT):
                        o = PB[po][:, kt * 128:(kt + 1) * 128]
                        mm(o, UY[:, g, kt * 128:(kt + 1) * 128], WTs[gi], True, False, [bUY[g], bWT[gi]], [PBb[po]])
                        mm(o, HB_[cf][cur[cf]][:, kt * 128:kt * 128 + 128], WYs[cf], False, False, [bH[cf][cur[cf]], bWs[cf]], [PBb[po]])
                        mm(o, HB_[cb][cur[cb]][:, kt * 128 + 2:kt * 128 + 130], WYs[cb], False, True, [bH[cb][cur[cb]], bWs[cb]], [PBb[po]])
                    if gi % 2 == 0:
                        act(UY[:, g, :], PB[po][:, 0:NCH], AF.Copy, [PBb[po]], [bUY[g]])
                    else:
                        cp(UY[:, g, :], PB[po][:, 0:NCH], [PBb[po]], [bUY[g]])
            if sq == 0 and dbg and "Y8" in dbg:
                dbgdump("Y8", UY[:].rearrange("p a b -> p (a b)")[:, 0:dbg["Y8"][1]], bUY)
            AR.release(mS)
            T.barrier()

            mB = AR.mark()
            X1 = AR.take([128, 4, D], F32)
            bX1 = [Buf("X1_%d" % i) for i in range(4)]
            HT = AR.take([128, 8, 512], BF16)
            bHT = Buf("HTb")
            YT = AR.take([128, 8, 512], BF16)
            bYT = Buf("YT")
            OT = AR.take([128, 8, 512], BF16)
            bOT = [Buf("OT%d" % i) for i in range(8)]
            QT = AR.take([128, 8, 512], BF16)
            bQT = Buf("QT")
            AT, bAT = QT, bQT
            NWB = 3
            WB = [AR.take([128, 8, 256], BF16) for _ in range(NWB)]
            bWB = [Buf("WB%d" % i) for i in range(NWB)]
            HBb = [AR.take([128, D], BF16) for _ in range(2)]
            bHBb = [Buf("HBb0"), Buf("HBb1")]
            SMB = [AR.take([128, 16], F32) for _ in range(2)]
            bSMB = [Buf("SMB0"), Buf("SMB1")]
            QSQ = [AR.take([128, 512], BF16) for _ in range(2)]
            QRS = [AR.take([128, 512], F32) for _ in range(2)]
            bQ = [Buf("qtmp0"), Buf("qtmp1")]
            PTR = [AR.take([128, 512], BF16) for _ in range(2)]
            bPTR = [Buf("PTR0"), Buf("PTR1")]
            PT = [AR.take([128, 3, 512], BF16) for _ in range(2)]
            bPT = [Buf("PT0"), Buf("PT1")]
            OTK = [AR.take([128, D], BF16) for _ in range(2)]
            bOTK = [Buf("OTK0"), Buf("OTK1")]
            DEN = [AR.take([128, 8], F32) for _ in range(2)]
            bDEN = [Buf("DEN0"), Buf("DEN1")]
            SGt = [AR.take([128, 512], BF16) for _ in range(2)]
            bSGt = [Buf("SGt0"), Buf("SGt1")]
            GAS = [AR.take([128, 512], BF16) for _ in range(2)]
            bGAS = [Buf("GS0"), Buf("GS1")]
            GA2 = [AR.take([128, 512], BF16) for _ in range(2)]
            bGA2 = [Buf("GA0"), Buf("GA1")]
            wslot = [0]
            scnt = [0]

            def wchunk(src_rows, c0, kt=8):
                i = wslot[0] % NWB
                wslot[0] += 1
                load_w(WB[i], src_rows, c0, 256, kt, bWB[i])
                return WB[i], bWB[i]

            def to_feature_major(hb, bh, dstT, bdst, b4, pbank, use_act=False):
                pt = pbf(pbank)
                for dt in range(8):
                    tr(pt[:, dt * 128:(dt + 1) * 128], hb[:, dt * 128:(dt + 1) * 128], IDB[:], [bh, bRES], [PBb[pbank]],
                       inc=(dt == 7))
                if use_act:
                    act(dstT[:, :, b4 * 128:(b4 + 1) * 128], pt.rearrange("p (a b) -> p a b", a=8), AF.Copy, [PBb[pbank]], bdst)
                else:
                    cp(dstT[:, :, b4 * 128:(b4 + 1) * 128], pt.rearrange("p (a b) -> p a b", a=8), [PBb[pbank]], bdst)

            slot_heads = [8 * a + 4 * hh + i for a in range(2) for i in range(4) for hh in range(2)]
            for tl in range(S // 512):
                kt, half = tl // 2, tl % 2
                t0 = tl * 512
                for b4 in range(4):
                    T.dma("sync", X1[:, b4, :], x_d[sq, t0 + b4 * 128:t0 + (b4 + 1) * 128, :], writes=[bX1[b4]])
                krow = slice(half * 64, (half + 1) * 64)
                for t in range(8):
                    gb = t % 2
                    yv = UY[krow, :, kt * 128 + t * 16:kt * 128 + (t + 1) * 16]
                    g2 = OTK[gb][krow, :].rearrange("p (g c) -> p g c", c=16)
                    act(g2, yv, AF.Gelu_apprx_tanh, bUY, [bOTK[gb]])
                    pbk = 5 + gb
                    pt = pbf(pbk)
                    for ct in range(8):
                        tr(pt[:, ct * 64:(ct + 1) * 64], OTK[gb][krow, ct * 128:(ct + 1) * 128], IDB[krow, half * 64:(half + 1) * 64],
                           [bOTK[gb], bRES], [PBb[pbk]], inc=(ct == 7))
                    if gb == 0:
                        cp(YT[:, :, t::8], pt[:, 0:512].rearrange("p (a b) -> p a b", a=8), [PBb[pbk]], [bYT])
                    else:
                        act(YT[:, :, t::8], pt[:, 0:512].rearrange("p (a b) -> p a b", a=8), AF.Copy, [PBb[pbk]], [bYT])
                for b4 in range(4):
                    rms_block(X1[:, b4, :], LN1[:], HBb[b4 % 2], SMB[b4 % 2], bX1[b4], bHBb[b4 % 2], bSMB[b4 % 2])
                    to_feature_major(HBb[b4 % 2], bHBb[b4 % 2], HT, [bHT], b4, 7 * (b4 % 2), use_act=(b4 % 2 == 1))
                krow = slice(half * 64, (half + 1) * 64)
                wq_of = {}

                def q_mm(m):
                    r = m % 2
                    pq = 1 if r == 0 else 3
                    if m % 2 == 0:
                        i = wslot[0] % NWB
                        wslot[0] += 1
                        for j in range(4):
                            h = slot_heads[2 * m + j]
                            T.dma("gpsimd", WB[i][:, :, j * 64:(j + 1) * 64],
                                  win_d[:, h * 64:(h + 1) * 64].rearrange("(k p) c -> p k c", p=128), writes=[bWB[i]])
                        wq_of[m] = wq_of[m + 1] = (WB[i], bWB[i])
                    wq, bwq = wq_of[m]
                    for dt in range(8):
                        mm(PB[pq][:], wq[:, dt, (m % 2) * 128:(m % 2 + 1) * 128], HT[:, dt, :], dt == 0, dt == 7,
                           [bwq, bHT], [PBb[pq]])
                    act(QSQ[r], PB[pq][:], AF.Square, [PBb[pq]], [bQ[r]])

                def q_norm(m):
                    r = m % 2
                    pq, pss = (1, 2) if r == 0 else (3, 4)
                    mm(PB[pss][:], BLK[:], QSQ[r], True, True, [bQ[r], bRES], [PBb[pss]])
                    act(QRS[r], PB[pss][:], AF.Ln, [PBb[pss]], [bQ[r]], scale=1.0 / 64, bias=EPS)
                    act(QRS[r], QRS[r], AF.Exp, [bQ[r]], [bQ[r]], scale=-0.5)
                    stt(QT[:, m, :], PB[pq][:], SM[:, 0:1], QRS[r], ALU.mult, ALU.mult, [PBb[pq], bQ[r], bRES], [bQT])

                q_mm(0)
                for m in range(8):
                    if m + 1 < 8:
                        q_mm(m + 1)
                    q_norm(m)
                pre = [wchunk(wglu_d, 0), wchunk(win_d, 2560), wchunk(win_d, 3584)]
                def att_scores(b4, hk):
                    blk = tl * 4 + b4
                    j, hh = hk // 2, hk % 2
                    r = hk % 2
                    prow = slice(hh * 64, (hh + 1) * 64)
                    bps = [bp for bp in range(3) if 0 <= blk - 1 + bp < NBLK]
                    for bp in bps:
                        kb = blk - 1 + bp
                        psb = 1 + scnt[0] % 4
                        x2 = scnt[0] % 2
                        scnt[0] += 1
                        mm(PB[psb][:], KT[prow, j, kb * 128:(kb + 1) * 128],
                           QT[prow, 4 * j:4 * j + 4, b4 * 128:(b4 + 1) * 128], True, True, [bKT, bQT], [PBb[psb]])
                        act(PTR[x2], PB[psb][:], AF.Exp, [PBb[psb]], [bPTR[x2]])
                        tt(PT[r][:, bp, :], PTR[x2], BE[:, bp, 4 * hk:4 * hk + 4, :].rearrange("p a b -> p (a b)"), ALU.mult,
                           [bPTR[x2], bRES], [bPT[r]], eng=("gpsimd" if bp == 2 else "vector"))

                def att_pv(b4, hk):
                    blk = tl * 4 + b4
                    ob = b4 % 2
                    r = hk % 2
                    pv = 5 + r
                    bps = [bp for bp in range(3) if 0 <= blk - 1 + bp < NBLK]
                    for i in range(4):
                        o = PB[pv][:, i * 65:(i + 1) * 65]
                        for n, bp in enumerate(bps):
                            kb = blk - 1 + bp
                            mm(o, PT[r][:, bp, i * 128:(i + 1) * 128], VS[:, kb, hk, :], n == 0, n == len(bps) - 1,
                               [bPT[r], bVS], [PBb[pv]])
                    o4 = PB[pv][:, 0:260].rearrange("p (a b) -> p a b", a=4)
                    tt(DEN[r][:, 0:4], o4[:, :, 64:65].rearrange("p a b -> p (a b)"), SM[:, 16 + 4 * hk:20 + 4 * hk], ALU.add,
                       [PBb[pv], bRES], [bDEN[r]])
                    vop(lambda E, r=r: E.reciprocal(DEN[r][:, 4:8], DEN[r][:, 0:4]), [bDEN[r]], [bDEN[r]])
                    tt(OTK[ob][:, hk * 256:(hk + 1) * 256].rearrange("p (a b) -> p a b", a=4), o4[:, :, 0:64],
                       DEN[r][:, 4:8].unsqueeze(2).to_broadcast([128, 4, 64]), ALU.mult, [PBb[pv], bDEN[r]], [bOTK[ob]])
                    if hk == 3:
                        to_feature_major(OTK[ob], bOTK[ob], OT, bOT, b4, 0 if ob == 0 else 7)

                steps = [(b4, hk) for b4 in range(4) for hk in range(4)]
                for n, (b4, hk) in enumerate(steps):
                    att_scores(b4, hk)
                    if n > 0:
                        att_pv(*steps[n - 1])
                att_pv(*steps[-1])
                nxt = pre
                for cp2 in range(4):
                    (wg, bwg), (wa, bwa), (ws, bws) = nxt
                    nxt = [None, None, None]
                    for r in range(2):
                        co = cp2 * 2 + r
                        pz = 1 + r
                        for ct in range(8):
                            mm(PB[pz][:], wg[:, ct, r * 128:(r + 1) * 128], YT[:, ct, :], ct == 0, ct == 7, [bwg, bYT], [PBb[pz]])
                        act(SGt[r], PB[pz][:], AF.Sigmoid, [PBb[pz], bRES], [bSGt[r]], bias=SM[:, 2 + co:3 + co])
                    if cp2 < 3:
                        nxt[0] = wchunk(wglu_d, (cp2 + 1) * 256)
                    for r in range(2):
                        pa = 3 + r
                        for ct in range(8):
                            mm(PB[pa][:], wa[:, ct, r * 128:(r + 1) * 128], HT[:, ct, :], ct == 0, ct == 7, [bwa, bHT], [PBb[pa]])
                        act(GA2[r], PB[pa][:], AF.Sigmoid, [PBb[pa]], [bGA2[r]])
                    if cp2 < 3:
                        nxt[1] = wchunk(win_d, 2560 + (cp2 + 1) * 256)
                    for r in range(2):
                        pg = 5 + r
                        for ct in range(8):
                            mm(PB[pg][:], ws[:, ct, r * 128:(r + 1) * 128], HT[:, ct, :], ct == 0, ct == 7, [bws, bHT], [PBb[pg]])
                        act(GAS[r], PB[pg][:], AF.Sigmoid, [PBb[pg]], [bGAS[r]])
                    if cp2 < 3:
                        nxt[2] = wchunk(win_d, 3584 + (cp2 + 1) * 256)
                    for r in range(2):
                        co = cp2 * 2 + r
                        tt(GA2[r], GA2[r], OT[:, co, :], ALU.mult, [bGA2[r], bOT[co]], [bGA2[r]])
                        tt(GAS[r], GAS[r], SGt[r], ALU.mult, [bGAS[r], bSGt[r]], [bGAS[r]])
                        tt(GAS[r], GAS[r], YT[:, co, :], ALU.mult, [bGAS[r], bYT], [bGAS[r]])
                        tt(OT[:, co, :], GA2[r], GAS[r], ALU.add, [bGA2[r], bGAS[r]], [bOT[co]])
                for cc in range(4):
                    wo, bwo = wchunk(wout_d, cc * 256)
                    for b4 in range(4):
                        pbk = 1 + b4 % 2
                        for ct in range(8):
                            mm(PB[pbk][:, 0:256], OT[:, ct, b4 * 128:(b4 + 1) * 128], wo[:, ct, :], ct == 0, ct == 7,
                               [bwo] + bOT, [PBb[pbk]])
                        tt(X1[:, b4, cc * 256:(cc + 1) * 256], PB[pbk][:, 0:256], X1[:, b4, cc * 256:(cc + 1) * 256], ALU.add,
                           [PBb[pbk], bX1[b4]], [bX1[b4]])
                for b4 in range(4):
                    rms_block(X1[:, b4, :], LN2[:], HBb[b4 % 2], SMB[b4 % 2], bX1[b4], bHBb[b4 % 2], bSMB[b4 % 2])
                    to_feature_major(HBb[b4 % 2], bHBb[b4 % 2], HT, [bHT], b4, 7 * (b4 % 2), use_act=(b4 % 2 == 1))
                for qf in range(4):
                    for fo in range(8):
                        if fo % 2 == 0:
                            wu, bwu = wchunk(wup_d, qf * 1024 + fo * 128)
                        pu = 1 + fo % 2
                        for dt in range(8):
                            mm(PB[pu][:], wu[:, dt, (fo % 2) * 128:(fo % 2 + 1) * 128], HT[:, dt, :], dt == 0, dt == 7,
                               [bwu, bHT], [PBb[pu]])
                        act(GAS[fo % 2], PB[pu][:], AF.Relu, [PBb[pu]], [bGAS[fo % 2]])
                        tt(AT[:, fo, :], GAS[fo % 2], GAS[fo % 2], ALU.mult, [bGAS[fo % 2]], [bAT])
                    for cc in range(4):
                        wd, bwd = wchunk(wdn_d[qf * 1024:(qf + 1) * 1024, :], cc * 256)
                        for b4 in range(4):
                            pbk = 3 + b4 % 2
                            for ft in range(8):
                                mm(PB[pbk][:, 0:256], AT[:, ft, b4 * 128:(b4 + 1) * 128], wd[:, ft, :], ft == 0, ft == 7,
                                   [bwd, bAT], [PBb[pbk]])
                            tt(X1[:, b4, cc * 256:(cc + 1) * 256], PB[pbk][:, 0:256], X1[:, b4, cc * 256:(cc + 1) * 256], ALU.add,
                               [PBb[pbk], bX1[b4]], [bX1[b4]])
                for b4 in range(4):
                    T.dma("sync", y_d[sq, t0 + b4 * 128:t0 + (b4 + 1) * 128, :], X1[:, b4, :], reads=[bX1[b4]])
            AR.release(mB)
            T.barrier()
        T.finish("sync")
        T.replay()
    return nc, T


_CACHE = {}


def _get_prog(nseq, s):
    key = (nseq, s)
    if key not in _CACHE:
        _CACHE[key] = build(nseq, s)[0]
    return _CACHE[key]


def kernel(x_prompt, x_sample, rel_table, ln1, w_in, q_gain, k_gain, sink, lam_re, lam_im, log_dt,
           b_re, b_im, c_re, c_im, d_skip, w_glu, b_glu, w_out, ln2, w_up, w_down):
    n = 8
    xs = np.concatenate([np.asarray(x_prompt, np.float32), np.asarray(x_sample, np.float32)], axis=0)
    nseq = xs.shape[0] // n
    S = xs.shape[1]
    nc = _get_prog(nseq, S)
    f = lambda a: np.ascontiguousarray(np.asarray(a, np.float32))
    shared = {
        "rel_table": f(rel_table), "ln1": f(ln1)[0], "w_in": f(w_in)[0], "q_gain": f(q_gain)[0], "k_gain": f(k_gain)[0],
        "sink": f(sink)[0], "lam_re": f(lam_re)[0], "lam_im": f(lam_im)[0], "log_dt": f(log_dt)[0],
        "b_re": f(b_re)[0], "b_im": f(b_im)[0], "c_re": f(c_re)[0], "c_im": f(c_im)[0], "d_skip": f(d_skip)[0],
        "w_glu": f(w_glu)[0], "b_glu": f(b_glu)[0], "w_out": f(w_out)[0], "ln2": f(ln2)[0], "w_up": f(w_up)[0],
        "w_down": f(w_down)[0],
    }
    shared.update(_consts())
    in_maps = []
    for c in range(n):
        m = dict(shared)
        m["x"] = np.ascontiguousarray(xs[c * nseq:(c + 1) * nseq])
        in_maps.append(m)
    res = run_bass_kernel_spmd(nc, in_maps, core_ids=list(range(n)))
    ys = np.concatenate([np.asarray(r["y"], np.float32) for r in res.results], axis=0)
    nb = x_prompt.shape[0]
    return (ys[:nb], ys[nb:])
```

```python
import contextlib
import math
import numpy as np
import concourse.bass as bass
import concourse.mybir as mybir
from concourse.bass_utils import run_bass_kernel_spmd

F32 = mybir.dt.float32
BF16 = mybir.dt.bfloat16
I32 = mybir.dt.int32
AF = mybir.ActivationFunctionType
ALU = mybir.AluOpType
AX = mybir.AxisListType

ENGS = ("tensor", "vector", "scalar", "gpsimd", "sync")

D = 1024
NG = 64
NP = 64
NC16 = 16
DFF = 4096
EPS = 1e-6
NEG = -30000.0


class Sem:
    def __init__(self, handle, inc, name):
        self.h = handle
        self.inc = inc
        self.n = 0
        self.name = name


class Buf:
    __slots__ = ("name", "w", "r")

    def __init__(self, name=""):
        self.name = name
        self.w = None
        self.r = {}


class Tracker:
    def __init__(self, nc, stack, n_dma_sems=8):
        self.nc = nc
        self.prog = {e: [] for e in ENGS}
        self.esem = {}
        for e in ENGS:
            self.esem[e] = Sem(stack.enter_context(nc.semaphore("s_" + e)), 1, e)
        self.seen = {e: {} for e in ENGS}
        self.dpool = {}
        self.dnext = {}
        for q in ("sync", "scalar", "gpsimd"):
            self.dpool[q] = [Sem(stack.enter_context(nc.semaphore("d_%s%d" % (q, i))), 16, "d_%s%d" % (q, i))
                             for i in range(n_dma_sems)]
            self.dnext[q] = 0
        self.pend = {e: ([], []) for e in ENGS}
        self.ninstr = 0

    def _wait(self, eng, sem, idx):
        if idx <= 0:
            return
        if self.seen[eng].get(sem, 0) >= idx:
            return
        self.seen[eng][sem] = idx
        h, v = sem.h, idx * sem.inc
        self.prog[eng].append(lambda E, h=h, v=v: E.wait_ge(h, v))
        self.ninstr += 1

    def _deps(self, eng, reads, writes):
        deps = {}

        def add(d):
            if d is None:
                return
            s, i = d
            if deps.get(s, 0) < i:
                deps[s] = i
        for b in reads:
            add(b.w)
        for b in writes:
            add(b.w)
            for s, i in b.r.items():
                add((s, i))
        for s, i in deps.items():
            self._wait(eng, s, i)

    def _check_pending(self, eng, reads, writes):
        for e in ENGS:
            if e == eng:
                continue
            pr, pw = self.pend[e]
            if not pr and not pw:
                continue
            for b in writes:
                assert b not in pr and b not in pw, "pending access on %s by %s" % (b.name, e)
            for b in reads:
                assert b not in pw, "pending write on %s by %s" % (b.name, e)

    def op(self, eng, fn, reads=(), writes=(), inc=True):
        reads = list(reads)
        writes = list(writes)
        self._check_pending(eng, reads, writes)
        self._deps(eng, reads, writes)
        sem = self.esem[eng]
        self.ninstr += 1
        if inc:
            sem.n += 1
            idx = sem.n
            h = sem.h
            self.prog[eng].append(lambda E, fn=fn, h=h: fn(E).then_inc(h, 1))
            pr, pw = self.pend[eng]
            for b in pw + writes:
                b.w = (sem, idx)
                b.r = {}
            for b in pr + reads:
                if b.w is not None and b.w[0] is sem and b.w[1] == idx:
                    continue
                b.r[sem] = idx
            self.pend[eng] = ([], [])
        else:
            self.prog[eng].append(lambda E, fn=fn: fn(E))
            pr, pw = self.pend[eng]
            pr.extend(reads)
            pw.extend(writes)

    def dma(self, q, out, in_, reads=(), writes=(), **kw):
        reads = list(reads)
        writes = list(writes)
        self._check_pending(q, reads, writes)
        pool = self.dpool[q]
        sem = pool[self.dnext[q] % len(pool)]
        self.dnext[q] += 1
        self._wait(q, sem, sem.n)
        self._deps(q, reads, writes)
        sem.n += 1
        idx = sem.n
        h = sem.h
        self.ninstr += 1
        self.prog[q].append(lambda E, out=out, in_=in_, h=h, kw=kw: E.dma_start(out=out, in_=in_, **kw).then_inc(h, 16))
        for b in writes:
            b.w = (sem, idx)
            b.r = {}
        for b in reads:
            b.r[sem] = idx

    def allsems(self):
        return list(self.esem.values()) + [s for p in self.dpool.values() for s in p]

    def barrier(self):
        for e in ENGS:
            assert not self.pend[e][0] and not self.pend[e][1]
        for e in ENGS:
            for s in self.allsems():
                if s is self.esem[e]:
                    continue
                self._wait(e, s, s.n)

    def finish(self, eng="sync"):
        for s in self.allsems():
            if s is self.esem[eng]:
                continue
            self._wait(eng, s, s.n)

    def replay(self):
        nc = self.nc
        with nc.Block() as block:
            @block.tensor
            def _(E):
                for f in self.prog["tensor"]:
                    f(E)

            @block.vector
            def _(E):
                for f in self.prog["vector"]:
                    f(E)

            @block.scalar
            def _(E):
                for f in self.prog["scalar"]:
                    f(E)

            @block.gpsimd
            def _(E):
                for f in self.prog["gpsimd"]:
                    f(E)

            @block.sync
            def _(E):
                for f in self.prog["sync"]:
                    f(E)


class Arena:
    def __init__(self, t, nbytes):
        self.t = t
        self.nbytes = nbytes
        self.off = 0

    def mark(self):
        return self.off

    def release(self, m):
        self.off = m

    def take(self, shape, dt):
        es = 4 if dt in (F32, I32) else 2
        n = int(np.prod(shape[1:]))
        nb = (n * es + 63) // 64 * 64
        assert self.off + nb <= self.nbytes, "arena overflow %d + %d > %d" % (self.off, nb, self.nbytes)
        v = self.t[0:int(shape[0]), self.off // 2:(self.off + n * es) // 2]
        self.off += nb
        if es == 4:
            v = v.bitcast(dt)
        if len(shape) > 2:
            names = " ".join("d%d" % i for i in range(len(shape) - 1))
            kw = {"d%d" % i: int(shape[i + 1]) for i in range(len(shape) - 1)}
            v = v.rearrange("p (%s) -> p %s" % (names, names), **kw)
        return v


def _t5_bucket(rel):
    nb = 16
    ret = (rel > 0).astype(np.int32) * nb
    n = np.abs(rel)
    max_exact = nb // 2
    n_safe = np.maximum(n, 1).astype(np.float32)
    large = max_exact + (np.log(n_safe / max_exact) / math.log(128 / max_exact) * (nb - max_exact)).astype(np.int32)
    large = np.minimum(large, nb - 1)
    return (ret + np.where(n < max_exact, n, large)).astype(np.int32)


def _consts():
    c = {}
    c["c_ident"] = np.eye(128, dtype=np.float32)
    sc = np.arange(128) // 16
    c["c_maskf"] = (sc[None, :] >= sc[:, None]).astype(np.float32)
    c["c_maskb"] = (sc[:, None] >= sc[None, :]).astype(np.float32)
    p = np.arange(128) % 64
    c["c_i2"] = (p[:, None] == np.arange(64)[None, :]).astype(np.float32)
    h = np.arange(128) // 64
    c["c_blk"] = (h[:, None] == h[None, :]).astype(np.float32)
    sg = np.where(np.arange(128) < 64, -1.0, 1.0).astype(np.float32)
    c["c_sgn"] = np.stack([sg, -sg], axis=1).astype(np.float32)
    j = np.arange(512)
    rel = j - 255
    bk = _t5_bucket(rel)
    oh = np.zeros((32, 512), np.float32)
    oh[bk[:511], j[:511]] = 1.0
    c["c_ohrev"] = oh
    key = np.arange(128)[:, None, None]
    bp = np.arange(3)[None, :, None]
    q = np.arange(128)[None, None, :]
    r = bp * 128 + key - 128 - q
    c["c_maskadd"] = np.where(np.abs(r) <= 128, 0.0, NEG).astype(np.float32)
    return c


def build(NSEQ, S, dbg=None):
    assert S % 1024 == 0
    NCH = S // 8
    NKT = S // 1024
    NBLK = S // 128
    NLEV = int(math.log2(NCH))
    assert 2 ** NLEV == NCH
    nc = bass.Bass("TRN2", target_bir_lowering=False)

    def din(name, shape, dt=F32):
        return nc.dram_tensor(name, list(shape), dt, kind="ExternalInput").ap()

    x_d = din("x", [NSEQ, S, D])
    y_d = nc.dram_tensor("y", [NSEQ, S, D], F32, kind="ExternalOutput").ap()
    rel_d = din("rel_table", [32, 16])
    ln1_d = din("ln1", [D])
    win_d = din("w_in", [D, 4608])
    qg_d = din("q_gain", [64])
    kg_d = din("k_gain", [64])
    sink_d = din("sink", [16])
    lre_d = din("lam_re", [2, NG, NP])
    lim_d = din("lam_im", [2, NG, NP])
    ldt_d = din("log_dt", [2, NG])
    bre_d = din("b_re", [2, NG, NP, NC16])
    bim_d = din("b_im", [2, NG, NP, NC16])
    cre_d = din("c_re", [2, NG, NC16, NP])
    cim_d = din("c_im", [2, NG, NC16, NP])
    dsk_d = din("d_skip", [D])
    wglu_d = din("w_glu", [D, D])
    bglu_d = din("b_glu", [D])
    wout_d = din("w_out", [D, D])
    ln2_d = din("ln2", [D])
    wup_d = din("w_up", [D, DFF])
    wdn_d = din("w_down", [DFF, D])
    cst = {k: din(k, v.shape) for k, v in _consts().items()}
    wx_s = nc.dram_tensor("wx_s", [128, 128, 128], BF16, kind="Internal").ap()
    wy_s = nc.dram_tensor("wy_s", [128, 128, 128], BF16, kind="Internal").ap()
    wt_s = nc.dram_tensor("wt_s", [64, 128, 128], BF16, kind="Internal").ap()
    vr_s = nc.dram_tensor("vr_s", [16, 512], F32, kind="Internal").ap()
    dbg_out = {}
    if dbg:
        for k, shp in dbg.items():
            dbg_out[k] = nc.dram_tensor("dbg_" + k, list(shp), F32, kind="ExternalOutput").ap()

    def bcast_rows(ap1d, n):
        return bass.AP(tensor=ap1d.tensor, offset=ap1d.offset, ap=[[0, 128], [1, n]])

    with contextlib.ExitStack() as st:
        T = Tracker(nc, st)

        def sb(name, shape, dt):
            return st.enter_context(nc.sbuf_tensor(name, list(shape), dt))

        def ps(name, shape, dt):
            return st.enter_context(nc.psum_tensor(name, list(shape), dt))

        UY = sb("UY", [128, NG, NCH], BF16)
        KT = sb("KT", [128, 2, S], BF16)
        VS = sb("VS", [128, NBLK, 4, 65], BF16)
        BE = sb("BE", [128, 3, 16, 128], BF16)
        WK = sb("WK", [128, 128, 9, 2], BF16)
        LN1 = sb("LN1", [128, D], BF16)
        LN2 = sb("LN2", [128, D], BF16)
        IDF = sb("IDF", [128, 128], F32)
        IDB = sb("IDB", [128, 128], BF16)
        I2 = sb("I2", [128, 64], F32)
        BLK = sb("BLK", [128, 128], BF16)
        SM = sb("SM", [128, 64], F32)
        ARENA_BYTES = 212863 - 1024 - (NG * NCH * 2 + 2 * S * 2 + NBLK * 4 * 65 * 2 + 3 * 16 * 128 * 2 + 128 * 18 * 2
                                       + 2 * D * 2 + 512 + 256 + 256 + 256 + 256)
        ARENA_BYTES = ARENA_BYTES // 64 * 64
        AR_T = sb("ARENA", [128, ARENA_BYTES // 2], BF16)
        AR = Arena(AR_T, ARENA_BYTES)
        PB = [ps("pb%d" % i, [128, 512], F32) for i in range(8)]
        PBb = [Buf("pb%d" % i) for i in range(8)]

        def pbf(i):
            return PB[i][:].bitcast(BF16)

        bUY = [Buf("UY%d" % g) for g in range(NG)]
        bKT = Buf("KT")
        bVS = Buf("VS")
        bRES = Buf("resident")

        def vop(fn, reads, writes, eng="vector"):
            T.op(eng, fn, reads, writes)

        def tt(out, a, b, op, reads, writes, eng="vector"):
            T.op(eng, lambda E: E.tensor_tensor(out, a, b, op), reads, writes)

        def ts(out, a, s1, s2, op0, op1, reads, writes, eng="vector"):
            if s2 is None:
                T.op(eng, lambda E: E.tensor_scalar(out, a, s1, None, op0), reads, writes)
            else:
                T.op(eng, lambda E: E.tensor_scalar(out, a, s1, s2, op0, op1), reads, writes)

        def stt(out, a, s, b, op0, op1, reads, writes):
            T.op("vector", lambda E: E.scalar_tensor_tensor(out, a, s, b, op0, op1), reads, writes)

        def cp(out, a, reads, writes, eng="vector"):
            T.op(eng, lambda E: E.tensor_copy(out, a), reads, writes)

        def act(out, a, func, reads, writes, **kw):
            T.op("scalar", lambda E: E.activation(out, a, func, **kw), reads, writes)

        def mm(out, lhsT, rhs, start, stop, reads, writes):
            T.op("tensor", lambda E: E.matmul(out, lhsT, rhs, start=start, stop=stop), reads, writes, inc=stop)

        def tr(out, in_, ident, reads, writes, inc=True):
            T.op("tensor", lambda E: E.transpose(out, in_, ident), reads, writes, inc=inc)

        def memset(ap, v, writes, eng="vector"):
            T.op(eng, lambda E: E.memset(ap, v), (), writes)

        def dbgdump(name, src_ap, reads):
            if name in dbg_out:
                T.dma("gpsimd", dbg_out[name], src_ap, reads=reads)

        bS = Buf("setup")
        R, W = [bS, bRES], [bS, bRES]
        m0 = AR.mark()
        T.dma("sync", IDF[:], cst["c_ident"], writes=W)
        T.dma("sync", I2[:], cst["c_i2"], writes=W)
        T.dma("gpsimd", IDB[:], cst["c_ident"], writes=W)
        T.dma("gpsimd", BLK[:], cst["c_blk"], writes=W)
        T.dma("gpsimd", LN1[:], bcast_rows(ln1_d, D), writes=W)
        T.dma("gpsimd", LN2[:], bcast_rows(ln2_d, D), writes=W)
        SGN = AR.take([128, 2], F32)
        T.dma("sync", SGN, cst["c_sgn"], writes=W)
        MKF = AR.take([128, 128], F32)
        MKB = AR.take([128, 128], F32)
        T.dma("sync", MKF, cst["c_maskf"], writes=W)
        T.dma("sync", MKB, cst["c_maskb"], writes=W)
        P1 = AR.take([128, 128, 8], F32)
        P2 = AR.take([128, 128, 8], F32)
        Q1y = AR.take([128, 128, 8], F32)
        Q2y = AR.take([128, 128, 8], F32)
        Q1h = AR.take([128, 128, 8], F32)
        Q2h = AR.take([128, 128, 8], F32)
        DV = AR.take([128, 64], F32)
        m1 = AR.mark()
        qg2 = bass.AP(tensor=qg_d.tensor, offset=0, ap=[[1, 64], [1, 1]])
        kg2 = bass.AP(tensor=kg_d.tensor, offset=0, ap=[[1, 64], [1, 1]])
        for hh in range(2):
            T.dma("sync", SM[hh * 64:(hh + 1) * 64, 0:1], qg2, writes=W)
            T.dma("sync", SM[hh * 64:(hh + 1) * 64, 1:2], kg2, writes=W)
        T.dma("sync", SM[:, 2:10], bglu_d.rearrange("(ct p) -> p ct", p=128), writes=W, allow_slow_non_contiguous=True)
        T.dma("sync", SM[:, 16:32], bcast_rows(sink_d, 16), writes=W)
        ts(SM[:, 0:1], SM[:, 0:1], 0.125, None, ALU.mult, None, R, W)
        act(SM[:, 16:32], SM[:, 16:32], AF.Exp, R, W)
        memset(VS[:, :, :, 64:65], 1.0, W)

        RT = AR.take([32, 16], F32)
        OH = AR.take([32, 512], F32)
        T.dma("sync", RT, rel_d, writes=W)
        T.dma("sync", OH, cst["c_ohrev"], writes=W)
        mm(PB[0][0:16, :], RT, OH, True, True, R, W + [PBb[0]])
        VR = AR.take([16, 512], F32)
        cp(VR, PB[0][0:16, :], R + [PBb[0]], W)
        T.dma("sync", vr_s, VR, reads=R, writes=W)
        MA = AR.take([128, 3, 128], F32)
        T.dma("sync", MA, cst["c_maskadd"], writes=W)
        BTMP = AR.take([128, 16, 128], F32)
        BTMP2 = AR.take([128, 16, 128], F32)
        for bp in range(3):
            src = bass.AP(tensor=vr_s.tensor, offset=bp * 128, ap=[[1, 128], [512, 16], [1, 128]])
            T.dma("sync", BTMP, src, reads=R, writes=W)
            tt(BTMP, BTMP[:, :, ::-1], MA[:, bp, :].unsqueeze(1).to_broadcast([128, 16, 128]), ALU.add, R, W) if False else None
            tt(BTMP2, BTMP[:, :, ::-1], MA[:, bp, :].unsqueeze(1).to_broadcast([128, 16, 128]), ALU.add, R, W)
            act(BE[:, bp, :, :], BTMP2, AF.Exp, R, W)

        AR.release(m1)
        def t128():
            return AR.take([128, 128], F32)
        LAM = AR.take([128, 2, 128], F32)
        lre2 = lre_d.rearrange("d g p -> (d g) p")
        lim2 = lim_d.rearrange("d g p -> (d g) p")
        for hh in range(2):
            T.dma("sync", LAM[:, 0, hh * 64:(hh + 1) * 64], lre2, writes=W)
            T.dma("sync", LAM[:, 1, hh * 64:(hh + 1) * 64], lim2, writes=W)
        LR, LI, DT = t128(), t128(), t128()
        mm(PB[0][:, 0:128], LAM[:, 0, :], IDF[:], True, True, R, W + [PBb[0]])
        cp(LR, PB[0][:, 0:128], R + [PBb[0]], W)
        mm(PB[1][:, 0:128], LAM[:, 1, :], IDF[:], True, True, R, W + [PBb[1]])
        cp(LI, PB[1][:, 0:128], R + [PBb[1]], W)
        T.dma("sync", DT, bcast_rows(ldt_d.rearrange("d g -> (d g)"), 128), writes=W)
        act(DT, DT, AF.Exp, R, W)
        LRDT, TU = t128(), t128()
        tt(LRDT, LR, DT, ALU.mult, R, W)
        tt(TU, LI, DT, ALU.mult, R, W)
        ts(TU, TU, 1.0 / (2 * math.pi), None, ALU.mult, None, R, W)
        MAG = t128()
        act(MAG, LRDT, AF.Exp, R, W)
        IM2 = t128()
        act(IM2, LRDT, AF.Exp, R, W, scale=-2.0)
        TI = AR.take([128, 128], I32)
        tA, tB, tC = t128(), t128(), t128()

        def reduce_turns(dst, src, shift):
            ts(tA, src, shift, None, ALU.add, None, R, W)
            cp(TI, tA, R, W)
            cp(tB, TI, R, W)
            tt(tA, tA, tB, ALU.subtract, R, W)
            ts(tB, tA, 0.5, None, ALU.is_gt, None, R, W)
            tt(tA, tA, tB, ALU.subtract, R, W)
            ts(tB, tA, -0.5, None, ALU.is_lt, None, R, W)
            tt(dst, tA, tB, ALU.add, R, W)

        A1R, A1I = t128(), t128()
        reduce_turns(tC, TU, 0.0)
        act(A1I, tC, AF.Sin, R, W, scale=2 * math.pi)
        reduce_turns(tC, TU, 0.25)
        act(A1R, tC, AF.Sin, R, W, scale=2 * math.pi)
        tt(A1R, A1R, MAG, ALU.mult, R, W)
        tt(A1I, A1I, MAG, ALU.mult, R, W)
        CBR, CBI = t128(), t128()
        ts(tA, A1R, -1.0, None, ALU.add, None, R, W)
        tt(tB, LR, LR, ALU.mult, R, W)
        tt(tC, LI, LI, ALU.mult, R, W)
        tt(tB, tB, tC, ALU.add, R, W)
        vop(lambda E: E.reciprocal(tB, tB), R, W)
        tt(CBR, tA, LR, ALU.mult, R, W)
        tt(tC, A1I, LI, ALU.mult, R, W)
        tt(CBR, CBR, tC, ALU.add, R, W)
        tt(CBR, CBR, tB, ALU.mult, R, W)
        tt(CBI, A1I, LR, ALU.mult, R, W)
        tt(tC, tA, LI, ALU.mult, R, W)
        tt(CBI, CBI, tC, ALU.subtract, R, W)
        tt(CBI, CBI, tB, ALU.mult, R, W)

        def cmul(zr, zi, xr, xi, yr, yi):
            tt(tA, xr, yr, ALU.mult, R, W)
            tt(tB, xi, yi, ALU.mult, R, W)
            tt(tC, xr, yi, ALU.mult, R, W)
            tt(zr, tA, tB, ALU.subtract, R, W)
            tt(tA, xi, yr, ALU.mult, R, W)
            tt(zi, tC, tA, ALU.add, R, W)

        PWR = AR.take([128, 16, 128], F32)
        PWI = AR.take([128, 16, 128], F32)

        def pw(n):
            return PWR[:, n + 7, :], PWI[:, n + 7, :]
        memset(PWR[:, 7, :], 1.0, W)
        memset(PWI[:, 7, :], 0.0, W)
        cp(PWR[:, 8, :], A1R, R, W)
        cp(PWI[:, 8, :], A1I, R, W)
        tt(PWR[:, 6, :], A1R, IM2, ALU.mult, R, W)
        tt(PWI[:, 6, :], A1I, IM2, ALU.mult, R, W)
        ts(PWI[:, 6, :], PWI[:, 6, :], -1.0, None, ALU.mult, None, R, W)
        for n in range(2, 9):
            cmul(*pw(n), *pw(n - 1), *pw(1))
        for n in range(2, 8):
            cmul(*pw(-n), *pw(-(n - 1)), *pw(-1))
        CLR, CLI, CNR, CNI = t128(), t128(), t128(), t128()
        cp(CLR, PWR[:, 15, :], R, W)
        cp(CLI, PWI[:, 15, :], R, W)
        for l in range(9):
            cp(WK[0:64, :, l, 0], CLR[0:64, :], R, W)
            ts(WK[64:128, :, l, 0], CLI[64:128, :], -1.0, None, ALU.mult, None, R, W)
            cp(WK[0:64, :, l, 1], CLI[0:64, :], R, W)
            cp(WK[64:128, :, l, 1], CLR[64:128, :], R, W)
            if l < 8:
                cmul(CNR, CNI, CLR, CLI, CLR, CLI)
                cp(CLR, CNR, R, W)
                cp(CLI, CNI, R, W)
        WXR, WXI = t128(), t128()
        for n in range(8):
            cmul(WXR, WXI, *pw(n), CBR, CBI)
            for (c0, c1, s) in ((0, 64, 7 - n), (64, 128, n)):
                cp(P1[:, c0:c1, s], WXR[:, c0:c1], R, W)
                ts(P2[:, c0:c1, s], WXI[:, c0:c1], SGN[:, 0:1], None, ALU.mult, None, R, W)
        for t in range(8):
            for (c0, c1, ey, eh) in ((0, 64, t + 1, t - 7), (64, 128, 8 - t, -t)):
                vr, vi = pw(ey)
                ts(Q1y[:, c0:c1, t], vr[:, c0:c1], SGN[:, 1:2], None, ALU.mult, None, R, W)
                ts(Q2y[:, c0:c1, t], vi[:, c0:c1], -1.0, None, ALU.mult, None, R, W)
                vr, vi = pw(eh)
                ts(Q1h[:, c0:c1, t], vr[:, c0:c1], SGN[:, 1:2], None, ALU.mult, None, R, W)
                ts(Q2h[:, c0:c1, t], vi[:, c0:c1], -1.0, None, ALU.mult, None, R, W)
        AR.release(m1)
        B1 = AR.take([128, 128, 16], F32)
        B2 = AR.take([128, 128, 16], F32)
        C1 = AR.take([128, 128, 16], F32)
        C2 = AR.take([128, 128, 16], F32)
        for d in range(2):
            sre = bre_d[d].rearrange("g p c -> p g c")
            sim = bim_d[d].rearrange("g p c -> p g c")
            gs = slice(d * 64, (d + 1) * 64)
            T.dma("sync", B1[0:64, gs, :], sre, writes=W)
            T.dma("sync", B1[64:128, gs, :], sim, writes=W)
            T.dma("sync", B2[0:64, gs, :], sim, writes=W)
            T.dma("sync", B2[64:128, gs, :], sre, writes=W)
        CN = AR.take([128, 16, 128], F32)
        cre2 = cre_d.rearrange("d (j g) c p -> (g c) (d j) p", g=8)
        cim2 = cim_d.rearrange("d (j g) c p -> (g c) (d j) p", g=8)
        for (CX, sa, sb_) in ((C1, cre2, cim2), (C2, cim2, cre2)):
            T.dma("sync", CN[:, :, 0:64], sa, writes=W)
            T.dma("sync", CN[:, :, 64:128], sb_, writes=W)
            for blk in range(16):
                pb = PB[blk % 2]
                mm(pb[:, 0:128], CN[:, blk, :], IDF[:], True, True, R, W + [PBb[blk % 2]])
                cp(CX[:, blk * 8:(blk + 1) * 8, :], pb[:, 0:128].rearrange("p (g c) -> p g c", g=8), R + [PBb[blk % 2]], W)
        dsk2 = dsk_d.rearrange("(g c) -> c g", c=16)
        for s in range(8):
            T.dma("sync", DV[s * 16:(s + 1) * 16, :], dsk2, writes=W, allow_slow_non_contiguous=True)
        NSL = 2
        XSs = [AR.take([128, 8, 16], F32) for _ in range(2 * NSL)]
        YSs = [AR.take([128, 8, 16], F32) for _ in range(2 * NSL)]
        YHs = [AR.take([128, 8, 16], F32) for _ in range(2 * NSL)]
        TMPs = [AR.take([128, 8, 16], F32) for _ in range(2 * NSL)]
        XTBs = [AR.take([128, 128], BF16) for _ in range(2 * NSL)]
        YSBs = [AR.take([128, 128], BF16) for _ in range(2 * NSL)]
        TFs = [AR.take([128, 128], F32) for _ in range(NSL)]
        TBs = [AR.take([128, 128], F32) for _ in range(NSL)]
        TTBs = [AR.take([128, 128], BF16) for _ in range(NSL)]
        bXS = [Buf("XS%d" % i) for i in range(2 * NSL)]
        bYS = [Buf("YS%d" % i) for i in range(2 * NSL)]
        bYH = [Buf("YH%d" % i) for i in range(2 * NSL)]
        bTMP = [Buf("TMP%d" % i) for i in range(2 * NSL)]
        bXTB = [Buf("XTB%d" % i) for i in range(2 * NSL)]
        bYSB = [Buf("YSB%d" % i) for i in range(2 * NSL)]
        bTF = [Buf("TF%d" % i) for i in range(NSL)]
        bSCRW = Buf("scrw")
        RT_ = [bS]

        def outer(dst, bdst, Pa, Ba, Pb, Bb, gd, tmp, btmp):
            tt(dst, Pa[:, gd, :].unsqueeze(2).to_broadcast([128, 8, 16]),
               Ba[:, gd, :].unsqueeze(1).to_broadcast([128, 8, 16]), ALU.mult, RT_, [bdst])
            tt(tmp, Pb[:, gd, :].unsqueeze(2).to_broadcast([128, 8, 16]),
               Bb[:, gd, :].unsqueeze(1).to_broadcast([128, 8, 16]), ALU.mult, RT_, [btmp])
            tt(dst, dst, tmp, ALU.add, [bdst, btmp], [bdst])

        def flat(v):
            return v.rearrange("p a b -> p (a b)")
        for g in range(NG):
            sl = g % NSL
            for d in range(2):
                gd = d * 64 + g
                k = sl * 2 + d
                outer(XSs[k], bXS[k], P1, B1, P2, B2, gd, TMPs[k], bTMP[k])
                outer(YSs[k], bYS[k], Q1y, C1, Q2y, C2, gd, TMPs[k], bTMP[k])
                outer(YHs[k], bYH[k], Q1h, C1, Q2h, C2, gd, TMPs[k], bTMP[k])
                pxt = k
                mm(PB[pxt][:, 0:128], flat(XSs[k]), IDF[:], True, True, [bXS[k]] + RT_, [PBb[pxt]])
                cp(XTBs[k], PB[pxt][:, 0:128], [PBb[pxt]], [bXTB[k]])
                T.dma("sync", wx_s[gd], XTBs[k], reads=[bXTB[k]], writes=[bSCRW])
                act(YSBs[k], flat(YSs[k]), AF.Copy, [bYS[k]], [bYSB[k]])
                T.dma("sync", wy_s[gd], YSBs[k], reads=[bYSB[k]], writes=[bSCRW])
                ptp = 4 + k
                mm(PB[ptp][:, 0:128], flat(XSs[k]), flat(YHs[k]), True, True, [bXS[k], bYH[k]], [PBb[ptp]])
            tt(TFs[sl], PB[4 + sl * 2][:, 0:128], MKF, ALU.mult, [PBb[4 + sl * 2]] + RT_, [bTF[sl]])
            tt(TBs[sl], PB[5 + sl * 2][:, 0:128], MKB, ALU.mult, [PBb[5 + sl * 2]] + RT_, [bTF[sl]])
            tt(TFs[sl], TFs[sl], TBs[sl], ALU.add, [bTF[sl]], [bTF[sl]])
            stt(TTBs[sl], IDF[:], DV[:, g:g + 1], TFs[sl], ALU.mult, ALU.add, [bTF[sl]] + RT_, [bTF[sl]])
            T.dma("sync", wt_s[g], TTBs[sl], reads=[bTF[sl]], writes=[bSCRW])
        AR.release(m0)
        T.barrier()
        bSCR = Buf("scratchw")

        def load_w(dst, src_rows, c0, ncols, kt, bw):
            T.dma("gpsimd", dst, src_rows[:, c0:c0 + ncols].rearrange("(k p) c -> p k c", p=128), writes=[bw])

        def rms_block(xt, lnb, hb, sm, bx, bh, bsm):
            memset(sm[:, 0:1], 0.0, [bsm])
            act(hb, xt, AF.Square, [bx], [bh, bsm], accum_out=sm[:, 0:1])
            act(sm[:, 1:2], sm[:, 0:1], AF.Ln, [bsm], [bsm], scale=1.0 / D, bias=EPS)
            act(sm[:, 2:3], sm[:, 1:2], AF.Exp, [bsm], [bsm], scale=-0.5)
            stt(hb, xt, sm[:, 2:3], lnb, ALU.mult, ALU.mult, [bx, bsm, bRES], [bh])

        for sq in range(NSEQ):
            mA = AR.mark()
            WU = AR.take([128, 8, 1024], BF16)
            WKV = AR.take([128, 8, 512], BF16)
            bWU, bWKV = Buf("WU"), Buf("WKV")
            load_w(WU, win_d, 1536, 1024, 8, bWU)
            load_w(WKV, win_d, 1024, 512, 8, bWKV)
            HTs = [AR.take([128, 8, 1024], BF16) for _ in range(2)]
            bHTs = [Buf("HTa0"), Buf("HTa1")]
            U8 = AR.take([128, NG, 8, 16], BF16)
            bU8 = Buf("U8")
            XB = [AR.take([128, D], F32) for _ in range(2)]
            bXB = [Buf("XB0"), Buf("XB1")]
            HB = [AR.take([128, D], BF16) for _ in range(2)]
            bHB = [Buf("HB0"), Buf("HB1")]
            SMA = [AR.take([128, 16], F32) for _ in range(2)]
            bSMA = [Buf("SMA0"), Buf("SMA1")]
            KSQ1 = AR.take([128, 512], BF16)
            KRS1 = AR.take([128, 512], F32)
            bK1 = Buf("ktmp")
            KSQ, KRS, bK = [KSQ1, KSQ1], [KRS1, KRS1], [bK1, bK1]

            def a_rms(kt, b8):
                blk = kt * 8 + b8
                i = blk % 2
                HT, bHT = HTs[kt % 2], bHTs[kt % 2]
                T.dma("sync", XB[i], x_d[sq, blk * 128:(blk + 1) * 128, :], writes=[bXB[i]])
                rms_block(XB[i], LN1[:], HB[i], SMA[i], bXB[i], bHB[i], bSMA[i])
                pt = pbf(0)
                for dt in range(8):
                    tr(pt[:, dt * 128:(dt + 1) * 128], HB[i][:, dt * 128:(dt + 1) * 128], IDB[:],
                       [bHB[i], bRES], [PBb[0]], inc=(dt == 7))
                cp(HT[:, :, b8 * 128:(b8 + 1) * 128], pt.rearrange("p (a b) -> p a b", a=8), [PBb[0]], [bHT])

            for b8 in range(8):
                a_rms(0, b8)
            for kt in range(NKT):
                HT, bHT = HTs[kt % 2], bHTs[kt % 2]
                for half in range(2):
                    tok = slice(half * 512, (half + 1) * 512)
                    gtok = slice(kt * 1024 + half * 512, kt * 1024 + (half + 1) * 512)
                    for j in range(2):
                        pk = 1 + j
                        for dt in range(8):
                            mm(PB[pk][:], WKV[:, dt, j * 128:(j + 1) * 128], HT[:, dt, tok], dt == 0, dt == 7,
                               [bWKV, bHT], [PBb[pk]])
                        act(KSQ[j], PB[pk][:], AF.Square, [PBb[pk]], [bK[j]])
                        mm(PB[3 + j][:], BLK[:], KSQ[j], True, True, [bK[j], bRES], [PBb[3 + j]])
                        act(KRS[j], PB[3 + j][:], AF.Ln, [PBb[3 + j]], [bK[j]], scale=1.0 / 64, bias=EPS)
                        act(KRS[j], KRS[j], AF.Exp, [bK[j]], [bK[j]], scale=-0.5)
                        stt(KT[:, j, gtok], PB[pk][:], SM[:, 1:2], KRS[j], ALU.mult, ALU.mult, [PBb[pk], bK[j], bRES], [bKT])
                    for b4 in range(4):
                        blk = kt * 8 + half * 4 + b4
                        tk = slice(half * 512 + b4 * 128, half * 512 + (b4 + 1) * 128)
                        pv_ = 5 + b4 % 2
                        for dt in range(8):
                            mm(PB[pv_][:, 0:256], HT[:, dt, tk], WKV[:, dt, 256:512], dt == 0, dt == 7,
                               [bWKV, bHT], [PBb[pv_]])
                        act(VS[:, blk, :, 0:64], PB[pv_][:, 0:256].rearrange("p (h d) -> p h d", h=4), AF.Copy,
                            [PBb[pv_]], [bVS])
                for s in range(8):
                    for half in range(2):
                        pu = 5 + (s * 2 + half) % 2
                        for dt in range(8):
                            mm(PB[pu][:], HT[:, dt, s::8], WU[:, dt, half * 512:(half + 1) * 512], dt == 0, dt == 7,
                               [bWU, bHT], [PBb[pu]])
                        dst = U8[:, half * 32:(half + 1) * 32, s, :]
                        src = PB[pu][:].rearrange("p (g c) -> p g c", c=16)
                        if (s * 2 + half) % 2 == 0:
                            act(dst, src, AF.Copy, [PBb[pu]], [bU8])
                        else:
                            cp(dst, src, [PBb[pu]], [bU8])
                    if kt + 1 < NKT:
                        a_rms(kt + 1, s)
                for g8 in range(8):
                    pt = pbf(7)
                    for gg in range(8):
                        g = g8 * 8 + gg
                        tr(pt[:, gg * 128:(gg + 1) * 128], U8[:, g, :, :].rearrange("p a b -> p (a b)"), IDB[:],
                           [bU8, bRES], [PBb[7]], inc=(gg == 7))
                    dst = UY[:, g8 * 8:(g8 + 1) * 8, kt * 128:(kt + 1) * 128]
                    src = pt.rearrange("p (a b) -> p a b", a=8)
                    if g8 % 2 == 0:
                        act(dst, src, AF.Copy, [PBb[7]], bUY[g8 * 8:(g8 + 1) * 8])
                    else:
                        cp(dst, src, [PBb[7]], bUY[g8 * 8:(g8 + 1) * 8])
            if sq == 0:
                dbgdump("UY", UY[:].rearrange("p a b -> p (a b)")[:, 0:dbg["UY"][1]] if dbg and "UY" in dbg else None, bUY)
                dbgdump("KT", KT[:].rearrange("p a b -> p (a b)"), [bKT])
            AR.release(mA)
            T.barrier()

            mS = AR.mark()
            NB = 2
            NSET = 2
            HBs = [[[AR.take([128, NCH + 2], BF16) for _ in range(2)] for _ in range(2 * NB)] for _ in range(NSET)]
            bHs = [[[Buf("H%d_%d_%d" % (st_, c, i)) for i in range(2)] for c in range(2 * NB)] for st_ in range(NSET)]
            WXss = [[AR.take([128, 128], BF16) for _ in range(2 * NB)] for _ in range(NSET)]
            WYss = [[AR.take([128, 128], BF16) for _ in range(2 * NB)] for _ in range(NSET)]
            WTss = [[AR.take([128, 128], BF16) for _ in range(NB)] for _ in range(NSET)]
            RTss = [[AR.take([128, 9, 2, 64], BF16) for _ in range(2 * NB)] for _ in range(NSET)]
            bWss = [[Buf("Wc%d_%d" % (st_, c)) for c in range(2 * NB)] for st_ in range(NSET)]
            bWTs = [[Buf("WT%d_%d" % (st_, c)) for c in range(NB)] for st_ in range(NSET)]
            bRTs = [[Buf("RT%d_%d" % (st_, c)) for c in range(2 * NB)] for st_ in range(NSET)]
            for st_ in range(NSET):
                for c in range(2 * NB):
                    for i in range(2):
                        memset(HBs[st_][c][i][:, 0:1], 0.0, [bHs[st_][c][i]])
                        memset(HBs[st_][c][i][:, NCH + 1:NCH + 2], 0.0, [bHs[st_][c][i]])

            def ssm_prefetch(bi):
                st_ = bi % NSET
                g0_ = bi * NB
                for gi in range(NB):
                    g = g0_ + gi
                    T.dma("sync", WTss[st_][gi], wt_s[g], reads=[bSCR], writes=[bWTs[st_][gi]])
                    for d in range(2):
                        c = gi * 2 + d
                        gd = d * 64 + g
                        T.dma("sync", WXss[st_][c], wx_s[gd], reads=[bSCR], writes=[bWss[st_][c]])
                        T.dma("sync", WYss[st_][c], wy_s[gd], reads=[bSCR], writes=[bWss[st_][c]])
                        tt(RTss[st_][c], I2[:].unsqueeze(1).unsqueeze(1).to_broadcast([128, 9, 2, 64]),
                           WK[:, gd, :, :].unsqueeze(3).to_broadcast([128, 9, 2, 64]), ALU.mult,
                           [bRES], [bRTs[st_][c]], eng="gpsimd")

            ssm_prefetch(0)
            for g0 in range(0, NG, NB):
                bi = g0 // NB
                st_ = bi % NSET
                HB_, bH = HBs[st_], bHs[st_]
                WXs, WYs, WTs, RTs = WXss[st_], WYss[st_], WTss[st_], RTss[st_]
                bWs, bWT, bRT = bWss[st_], bWTs[st_], bRTs[st_]
                chains = [(gi * 2 + d, g0 + gi, d) for gi in range(NB) for d in range(2)]
                for (c, g, d) in chains:
                    mm(PB[c][:, 0:NCH], WXs[c], UY[:, g, :], True, True, [bWs[c], bUY[g]], [PBb[c]])
                if g0 + NB < NG:
                    ssm_prefetch(bi + 1)
                for (c, g, d) in chains:
                    if c % 2 == 0:
                        act(HB_[c][0][:, 1:NCH + 1], PB[c][:, 0:NCH], AF.Copy, [PBb[c]], [bH[c][0]])
                    else:
                        cp(HB_[c][0][:, 1:NCH + 1], PB[c][:, 0:NCH], [PBb[c]], [bH[c][0]])
                cur = [0] * (2 * NB)
                for l in range(NLEV):
                    m = 1 << l
                    for (c, g, d) in chains:
                        Hc = HB_[c][cur[c]]
                        rt = RTs[c][:, l, :, :].rearrange("p a b -> p (a b)")
                        if c % 2 == 0:
                            mm(PB[c][:, 0:NCH], IDB[:], Hc[:, 1:NCH + 1], True, False, [bRES, bH[c][cur[c]]], [PBb[c]])
                            if d == 0:
                                mm(PB[c][:, m:NCH], rt, Hc[:, 1:NCH + 1 - m], False, True, [bRT[c], bH[c][cur[c]]], [PBb[c]])
                            else:
                                mm(PB[c][:, 0:NCH - m], rt, Hc[:, 1 + m:NCH + 1], False, True, [bRT[c], bH[c][cur[c]]], [PBb[c]])
                        else:
                            if d == 0:
                                mm(PB[c][:, m:NCH], rt, Hc[:, 1:NCH + 1 - m], True, True, [bRT[c], bH[c][cur[c]]], [PBb[c]])
                            else:
                                mm(PB[c][:, 0:NCH - m], rt, Hc[:, 1 + m:NCH + 1], True, True, [bRT[c], bH[c][cur[c]]], [PBb[c]])
                    for (c, g, d) in chains:
                        if c % 2 == 0:
                            Hn = HB_[c][1 - cur[c]]
                            act(Hn[:, 1:NCH + 1], PB[c][:, 0:NCH], AF.Copy, [PBb[c]], [bH[c][1 - cur[c]]])
                            cur[c] = 1 - cur[c]
                        else:
                            Hc = HB_[c][cur[c]]
                            if d == 0:
                                tt(Hc[:, 1 + m:NCH + 1], PB[c][:, m:NCH], Hc[:, 1 + m:NCH + 1], ALU.add,
                                   [PBb[c], bH[c][cur[c]]], [bH[c][cur[c]]])
                            else:
                                tt(Hc[:, 1:NCH + 1 - m], PB[c][:, 0:NCH - m], Hc[:, 1:NCH + 1 - m], ALU.add,
                                   [PBb[c], bH[c][cur[c]]], [bH[c][cur[c]]])
                for gi in range(NB):
                    g = g0 + gi
                    cf, cb = gi * 2, gi * 2 + 1
                    po = 4 + gi
                    for kt in range(NKT):
                        o = PB[po][:, kt * 128:(kt + 1) * 128]
                        mm(o, UY[:, g, kt * 128:(kt + 1) * 128], WTs[gi], True, False, [bUY[g], bWT[gi]], [PBb[po]])
                        mm(o, HB_[cf][cur[cf]][:, kt * 128:kt * 128 + 128], WYs[cf], False, False, [bH[cf][cur[cf]], bWs[cf]], [PBb[po]])
                        mm(o, HB_[cb][cur[cb]][:, kt * 128 + 2:kt * 128 + 130], WYs[cb], False, True, [bH[cb][cur[cb]], bWs[cb]], [PBb[po]])
                    if gi % 2 == 0:
                        act(UY[:, g, :], PB[po][:, 0:NCH], AF.Copy, [PBb[po]], [bUY[g]])
                    else:
                        cp(UY[:, g, :], PB[po][:, 0:NCH], [PBb[po]], [bUY[g]])
            if sq == 0 and dbg and "Y8" in dbg:
                dbgdump("Y8", UY[:].rearrange("p a b -> p (a b)")[:, 0:dbg["Y8"][1]], bUY)
            AR.release(mS)
            T.barrier()

            mB = AR.mark()
            X1 = AR.take([128, 4, D], F32)
            bX1 = [Buf("X1_%d" % i) for i in range(4)]
            HT = AR.take([128, 8, 512], BF16)
            bHT = Buf("HTb")
            YT = AR.take([128, 8, 512], BF16)
            bYT = Buf("YT")
            OT = AR.take([128, 8, 512], BF16)
            bOT = [Buf("OT%d" % i) for i in range(8)]
            QT = AR.take([128, 8, 512], BF16)
            bQT = Buf("QT")
            AT, bAT = QT, bQT
            NWB = 3
            WB = [AR.take([128, 8, 256], BF16) for _ in range(NWB)]
            bWB = [Buf("WB%d" % i) for i in range(NWB)]
            HBb = [AR.take([128, D], BF16) for _ in range(2)]
            bHBb = [Buf("HBb0"), Buf("HBb1")]
            SMB = [AR.take([128, 16], F32) for _ in range(2)]
            bSMB = [Buf("SMB0"), Buf("SMB1")]
            QSQ = [AR.take([128, 512], BF16) for _ in range(2)]
            QRS = [AR.take([128, 512], F32) for _ in range(2)]
            bQ = [Buf("qtmp0"), Buf("qtmp1")]
            PTR = [AR.take([128, 512], BF16) for _ in range(2)]
            bPTR = [Buf("PTR0"), Buf("PTR1")]
            PT = [AR.take([128, 3, 512], BF16) for _ in range(2)]
            bPT = [Buf("PT0"), Buf("PT1")]
            OTK = [AR.take([128, D], BF16) for _ in range(2)]
            bOTK = [Buf("OTK0"), Buf("OTK1")]
            DEN = [AR.take([128, 8], F32) for _ in range(2)]
            bDEN = [Buf("DEN0"), Buf("DEN1")]
            SGt = [AR.take([128, 512], BF16) for _ in range(2)]
            bSGt = [Buf("SGt0"), Buf("SGt1")]
            GAS = [AR.take([128, 512], BF16) for _ in range(2)]
            bGAS = [Buf("GS0"), Buf("GS1")]
            GA2 = [AR.take([128, 512], BF16) for _ in range(2)]
            bGA2 = [Buf("GA0"), Buf("GA1")]
            wslot = [0]
            scnt = [0]

            def wchunk(src_rows, c0, kt=8):
                i = wslot[0] % NWB
                wslot[0] += 1
                load_w(WB[i], src_rows, c0, 256, kt, bWB[i])
                return WB[i], bWB[i]

            def to_feature_major(hb, bh, dstT, bdst, b4, pbank, use_act=False):
                pt = pbf(pbank)
                for dt in range(8):
                    tr(pt[:, dt * 128:(dt + 1) * 128], hb[:, dt * 128:(dt + 1) * 128], IDB[:], [bh, bRES], [PBb[pbank]],
                       inc=(dt == 7))
                if use_act:
                    act(dstT[:, :, b4 * 128:(b4 + 1) * 128], pt.rearrange("p (a b) -> p a b", a=8), AF.Copy, [PBb[pbank]], bdst)
                else:
                    cp(dstT[:, :, b4 * 128:(b4 + 1) * 128], pt.rearrange("p (a b) -> p a b", a=8), [PBb[pbank]], bdst)

            slot_heads = [8 * a + 4 * hh + i for a in range(2) for i in range(4) for hh in range(2)]
            for tl in range(S // 512):
                kt, half = tl // 2, tl % 2
                t0 = tl * 512
                for b4 in range(4):
                    T.dma("sync", X1[:, b4, :], x_d[sq, t0 + b4 * 128:t0 + (b4 + 1) * 128, :], writes=[bX1[b4]])
                krow = slice(half * 64, (half + 1) * 64)
                for t in range(8):
                    gb = t % 2
                    yv = UY[krow, :, kt * 128 + t * 16:kt * 128 + (t + 1) * 16]
                    g2 = OTK[gb][krow, :].rearrange("p (g c) -> p g c", c=16)
                    act(g2, yv, AF.Gelu_apprx_tanh, bUY, [bOTK[gb]])
                    pbk = 5 + gb
                    pt = pbf(pbk)
                    for ct in range(8):
                        tr(pt[:, ct * 64:(ct + 1) * 64], OTK[gb][krow, ct * 128:(ct + 1) * 128], IDB[krow, half * 64:(half + 1) * 64],
                           [bOTK[gb], bRES], [PBb[pbk]], inc=(ct == 7))
                    if gb == 0:
                        cp(YT[:, :, t::8], pt[:, 0:512].rearrange("p (a b) -> p a b", a=8), [PBb[pbk]], [bYT])
                    else:
                        act(YT[:, :, t::8], pt[:, 0:512].rearrange("p (a b) -> p a b", a=8), AF.Copy, [PBb[pbk]], [bYT])
                for b4 in range(4):
                    rms_block(X1[:, b4, :], LN1[:], HBb[b4 % 2], SMB[b4 % 2], bX1[b4], bHBb[b4 % 2], bSMB[b4 % 2])
                    to_feature_major(HBb[b4 % 2], bHBb[b4 % 2], HT, [bHT], b4, 7 * (b4 % 2), use_act=(b4 % 2 == 1))
                krow = slice(half * 64, (half + 1) * 64)
                wq_of = {}

                def q_mm(m):
                    r = m % 2
                    pq = 1 if r == 0 else 3
                    if m % 2 == 0:
                        i = wslot[0] % NWB
                        wslot[0] += 1
                        for j in range(4):
                            h = slot_heads[2 * m + j]
                            T.dma("gpsimd", WB[i][:, :, j * 64:(j + 1) * 64],
                                  win_d[:, h * 64:(h + 1) * 64].rearrange("(k p) c -> p k c", p=128), writes=[bWB[i]])
                        wq_of[m] = wq_of[m + 1] = (WB[i], bWB[i])
                    wq, bwq = wq_of[m]
                    for dt in range(8):
                        mm(PB[pq][:], wq[:, dt, (m % 2) * 128:(m % 2 + 1) * 128], HT[:, dt, :], dt == 0, dt == 7,
                           [bwq, bHT], [PBb[pq]])
                    act(QSQ[r], PB[pq][:], AF.Square, [PBb[pq]], [bQ[r]])

                def q_norm(m):
                    r = m % 2
                    pq, pss = (1, 2) if r == 0 else (3, 4)
                    mm(PB[pss][:], BLK[:], QSQ[r], True, True, [bQ[r], bRES], [PBb[pss]])
                    act(QRS[r], PB[pss][:], AF.Ln, [PBb[pss]], [bQ[r]], scale=1.0 / 64, bias=EPS)
                    act(QRS[r], QRS[r], AF.Exp, [bQ[r]], [bQ[r]], scale=-0.5)
                    stt(QT[:, m, :], PB[pq][:], SM[:, 0:1], QRS[r], ALU.mult, ALU.mult, [PBb[pq], bQ[r], bRES], [bQT])

                q_mm(0)
                for m in range(8):
                    if m + 1 < 8:
                        q_mm(m + 1)
                    q_norm(m)
                pre = [wchunk(wglu_d, 0), wchunk(win_d, 2560), wchunk(win_d, 3584)]
                def att_scores(b4, hk):
                    blk = tl * 4 + b4
                    j, hh = hk // 2, hk % 2
                    r = hk % 2
                    prow = slice(hh * 64, (hh + 1) * 64)
                    bps = [bp for bp in range(3) if 0 <= blk - 1 + bp < NBLK]
                    for bp in bps:
                        kb = blk - 1 + bp
                        psb = 1 + scnt[0] % 4
                        x2 = scnt[0] % 2
                        scnt[0] += 1
                        mm(PB[psb][:], KT[prow, j, kb * 128:(kb + 1) * 128],
                           QT[prow, 4 * j:4 * j + 4, b4 * 128:(b4 + 1) * 128], True, True, [bKT, bQT], [PBb[psb]])
                        act(PTR[x2], PB[psb][:], AF.Exp, [PBb[psb]], [bPTR[x2]])
                        tt(PT[r][:, bp, :], PTR[x2], BE[:, bp, 4 * hk:4 * hk + 4, :].rearrange("p a b -> p (a b)"), ALU.mult,
                           [bPTR[x2], bRES], [bPT[r]], eng=("gpsimd" if bp == 2 else "vector"))

                def att_pv(b4, hk):
                    blk = tl * 4 + b4
                    ob = b4 % 2
                    r = hk % 2
                    pv = 5 + r
                    bps = [bp for bp in range(3) if 0 <= blk - 1 + bp < NBLK]
                    for i in range(4):
                        o = PB[pv][:, i * 65:(i + 1) * 65]
                        for n, bp in enumerate(bps):
                            kb = blk - 1 + bp
                            mm(o, PT[r][:, bp, i * 128:(i + 1) * 128], VS[:, kb, hk, :], n == 0, n == len(bps) - 1,
                               [bPT[r], bVS], [PBb[pv]])
                    o4 = PB[pv][:, 0:260].rearrange("p (a b) -> p a b", a=4)
                    tt(DEN[r][:, 0:4], o4[:, :, 64:65].rearrange("p a b -> p (a b)"), SM[:, 16 + 4 * hk:20 + 4 * hk], ALU.add,
                       [PBb[pv], bRES], [bDEN[r]])
                    vop(lambda E, r=r: E.reciprocal(DEN[r][:, 4:8], DEN[r][:, 0:4]), [bDEN[r]], [bDEN[r]])
                    tt(OTK[ob][:, hk * 256:(hk + 1) * 256].rearrange("p (a b) -> p a b", a=4), o4[:, :, 0:64],
                       DEN[r][:, 4:8].unsqueeze(2).to_broadcast([128, 4, 64]), ALU.mult, [PBb[pv], bDEN[r]], [bOTK[ob]])
                    if hk == 3:
                        to_feature_major(OTK[ob], bOTK[ob], OT, bOT, b4, 0 if ob == 0 else 7)

                steps = [(b4, hk) for b4 in range(4) for hk in range(4)]
                for n, (b4, hk) in enumerate(steps):
                    att_scores(b4, hk)
                    if n > 0:
                        att_pv(*steps[n - 1])
                att_pv(*steps[-1])
                nxt = pre
                for cp2 in range(4):
                    (wg, bwg), (wa, bwa), (ws, bws) = nxt
                    nxt = [None, None, None]
                    for r in range(2):
                        co = cp2 * 2 + r
                        pz = 1 + r
                        for ct in range(8):
                            mm(PB[pz][:], wg[:, ct, r * 128:(r + 1) * 128], YT[:, ct, :], ct == 0, ct == 7, [bwg, bYT], [PBb[pz]])
                        act(SGt[r], PB[pz][:], AF.Sigmoid, [PBb[pz], bRES], [bSGt[r]], bias=SM[:, 2 + co:3 + co])
                    if cp2 < 3:
                        nxt[0] = wchunk(wglu_d, (cp2 + 1) * 256)
                    for r in range(2):
                        pa = 3 + r
                        for ct in range(8):
                            mm(PB[pa][:], wa[:, ct, r * 128:(r + 1) * 128], HT[:, ct, :], ct == 0, ct == 7, [bwa, bHT], [PBb[pa]])
                        act(GA2[r], PB[pa][:], AF.Sigmoid, [PBb[pa]], [bGA2[r]])
                    if cp2 < 3:
                        nxt[1] = wchunk(win_d, 2560 + (cp2 + 1) * 256)
                    for r in range(2):
                        pg = 5 + r
                        for ct in range(8):
                            mm(PB[pg][:], ws[:, ct, r * 128:(r + 1) * 128], HT[:, ct, :], ct == 0, ct == 7, [bws, bHT], [PBb[pg]])
                        act(GAS[r], PB[pg][:], AF.Sigmoid, [PBb[pg]], [bGAS[r]])
                    if cp2 < 3:
                        nxt[2] = wchunk(win_d, 3584 + (cp2 + 1) * 256)
                    for r in range(2):
                        co = cp2 * 2 + r
                        tt(GA2[r], GA2[r], OT[:, co, :], ALU.mult, [bGA2[r], bOT[co]], [bGA2[r]])
                        tt(GAS[r], GAS[r], SGt[r], ALU.mult, [bGAS[r], bSGt[r]], [bGAS[r]])
                        tt(GAS[r], GAS[r], YT[:, co, :], ALU.mult, [bGAS[r], bYT], [bGAS[r]])
                        tt(OT[:, co, :], GA2[r], GAS[r], ALU.add, [bGA2[r], bGAS[r]], [bOT[co]])
                for cc in range(4):
                    wo, bwo = wchunk(wout_d, cc * 256)
                    for b4 in range(4):
                        pbk = 1 + b4 % 2
                        for ct in range(8):
                            mm(PB[pbk][:, 0:256], OT[:, ct, b4 * 128:(b4 + 1) * 128], wo[:, ct, :], ct == 0, ct == 7,
                               [bwo] + bOT, [PBb[pbk]])
                        tt(X1[:, b4, cc * 256:(cc + 1) * 256], PB[pbk][:, 0:256], X1[:, b4, cc * 256:(cc + 1) * 256], ALU.add,
                           [PBb[pbk], bX1[b4]], [bX1[b4]])
                for b4 in range(4):
                    rms_block(X1[:, b4, :], LN2[:], HBb[b4 % 2], SMB[b4 % 2], bX1[b4], bHBb[b4 % 2], bSMB[b4 % 2])
                    to_feature_major(HBb[b4 % 2], bHBb[b4 % 2], HT, [bHT], b4, 7 * (b4 % 2), use_act=(b4 % 2 == 1))
                for qf in range(4):
                    for fo in range(8):
                        if fo % 2 == 0:
                            wu, bwu = wchunk(wup_d, qf * 1024 + fo * 128)
                        pu = 1 + fo % 2
                        for dt in range(8):
                            mm(PB[pu][:], wu[:, dt, (fo % 2) * 128:(fo % 2 + 1) * 128], HT[:, dt, :], dt == 0, dt == 7,
                               [bwu, bHT], [PBb[pu]])
                        act(GAS[fo % 2], PB[pu][:], AF.Relu, [PBb[pu]], [bGAS[fo % 2]])
                        tt(AT[:, fo, :], GAS[fo % 2], GAS[fo % 2], ALU.mult, [bGAS[fo % 2]], [bAT])
                    for cc in range(4):
                        wd, bwd = wchunk(wdn_d[qf * 1024:(qf + 1) * 1024, :], cc * 256)
                        for b4 in range(4):
                            pbk = 3 + b4 % 2
                            for ft in range(8):
                                mm(PB[pbk][:, 0:256], AT[:, ft, b4 * 128:(b4 + 1) * 128], wd[:, ft, :], ft == 0, ft == 7,
                                   [bwd, bAT], [PBb[pbk]])
                            tt(X1[:, b4, cc * 256:(cc + 1) * 256], PB[pbk][:, 0:256], X1[:, b4, cc * 256:(cc + 1) * 256], ALU.add,
                               [PBb[pbk], bX1[b4]], [bX1[b4]])
                for b4 in range(4):
                    T.dma("sync", y_d[sq, t0 + b4 * 128:t0 + (b4 + 1) * 128, :], X1[:, b4, :], reads=[bX1[b4]])
            AR.release(mB)
            T.barrier()
        T.finish("sync")
        T.replay()
    return nc, T


_CACHE = {}


def _get_prog(nseq, s):
    key = (nseq, s)
    if key not in _CACHE:
        _CACHE[key] = build(nseq, s)[0]
    return _CACHE[key]


def kernel(x_prompt, x_sample, rel_table, ln1, w_in, q_gain, k_gain, sink, lam_re, lam_im, log_dt,
           b_re, b_im, c_re, c_im, d_skip, w_glu, b_glu, w_out, ln2, w_up, w_down):
    n = 8
    xs = np.concatenate([np.asarray(x_prompt, np.float32), np.asarray(x_sample, np.float32)], axis=0)
    nseq = xs.shape[0] // n
    S = xs.shape[1]
    nc = _get_prog(nseq, S)
    f = lambda a: np.ascontiguousarray(np.asarray(a, np.float32))
    shared = {
        "rel_table": f(rel_table), "ln1": f(ln1)[0], "w_in": f(w_in)[0], "q_gain": f(q_gain)[0], "k_gain": f(k_gain)[0],
        "sink": f(sink)[0], "lam_re": f(lam_re)[0], "lam_im": f(lam_im)[0], "log_dt": f(log_dt)[0],
        "b_re": f(b_re)[0], "b_im": f(b_im)[0], "c_re": f(c_re)[0], "c_im": f(c_im)[0], "d_skip": f(d_skip)[0],
        "w_glu": f(w_glu)[0], "b_glu": f(b_glu)[0], "w_out": f(w_out)[0], "ln2": f(ln2)[0], "w_up": f(w_up)[0],
        "w_down": f(w_down)[0],
    }
    shared.update(_consts())
    in_maps = []
    for c in range(n):
        m = dict(shared)
        m["x"] = np.ascontiguousarray(xs[c * nseq:(c + 1) * nseq])
        in_maps.append(m)
    res = run_bass_kernel_spmd(nc, in_maps, core_ids=list(range(n)))
    ys = np.concatenate([np.asarray(r["y"], np.float32) for r in res.results], axis=0)
    nb = x_prompt.shape[0]
    return (ys[:nb], ys[nb:])
```
